# Optimizing a Trainium2 kernel written in Bass

```python
import math
import jax, jax.numpy as jnp
from jax import lax
import numpy as np

D_MODEL = 2048
BATCH = 16
SEQ = 256
DEPTH = 2
DEC_BATCH = 2
DEC_SEQ = 2048
PAST_LEN = 256

GRID_W = 64
N_MIXERS = 2
N_A_LAYERS = (DEPTH + N_MIXERS - 1) // N_MIXERS
N_B_LAYERS = DEPTH // N_MIXERS
N_SUB = 3
A_HEAD_DIM = 128
A_HEADS = D_MODEL // A_HEAD_DIM
A_KV_HEADS = 4
B_QK_DIM = 64
B_V_DIM = 2 * B_QK_DIM
B_HEADS = D_MODEL // B_V_DIM
D_FF = ((8 * D_MODEL // 3 + 127) // 128) * 128
ROPE_THETA = 10000.0
Q_BLOCK = 128
EPS = 1e-6
MACARON_WEIGHT = 0.5
DEEPNORM_ALPHA = (2 * DEPTH) ** 0.25
DEEPNORM_BETA = (8 * DEPTH) ** -0.25

kernel_name = "hybrid_diffusion_gqa_diffattn_macaron_step"


def layer_norm(x, g, b):
    xf = x.astype(jnp.float32)
    mu = jnp.mean(xf, -1, keepdims=True)
    xc = xf - mu
    var = jnp.mean(xc * xc, -1, keepdims=True)
    return (xc * lax.rsqrt(var + EPS) * g + b).astype(x.dtype)


def rms_norm(x, g):
    xf = x.astype(jnp.float32)
    return (xf * lax.rsqrt(jnp.mean(xf * xf, -1, keepdims=True) + EPS) * g).astype(x.dtype)


def modulation(cond, w, b):
    m = jax.nn.silu(cond) @ w + b
    return m.reshape(cond.shape[0], N_SUB, 3, D_MODEL)


def modulate(x, m, s):
    shift = m[:, s, 0][:, None, :]
    scale = m[:, s, 1][:, None, :]
    gate = m[:, s, 2][:, None, :]
    return x * (1 + scale) + shift, gate


def post_norm(x, out, g, b):
    return layer_norm(DEEPNORM_ALPHA * x + out, g, b)


def swiglu(h, w_in, w_out):
    gu = h @ w_in
    g, u = jnp.split(gu, 2, axis=-1)
    return (jax.nn.silu(g) * u) @ w_out


def ffn_sublayer(x, m, s, w_in, w_out, g, b):
    h, gate = modulate(x, m, s)
    return post_norm(x, MACARON_WEIGHT * gate * swiglu(h, w_in, w_out), g, b)


def axial_rope(rows, dim):
    row = jnp.repeat(jnp.arange(rows, dtype=jnp.float32), GRID_W)
    col = jnp.tile(jnp.arange(GRID_W, dtype=jnp.float32), rows)
    quarter = dim // 4
    inv = ROPE_THETA ** (-jnp.arange(quarter, dtype=jnp.float32) / quarter)
    ang = jnp.concatenate([row[:, None] * inv, col[:, None] * inv], axis=-1)
    return jnp.cos(ang), jnp.sin(ang)


def apply_rope(x, cos, sin):
    shp = (cos.shape[0],) + (1,) * (x.ndim - 3) + (cos.shape[-1],)
    cos = cos.reshape(shp)
    sin = sin.reshape(shp)
    xf = x.astype(jnp.float32)
    half = x.shape[-1] // 2
    x1, x2 = xf[..., :half], xf[..., half:]
    return jnp.concatenate([x1 * cos - x2 * sin, x2 * cos + x1 * sin], axis=-1).astype(x.dtype)


def split_query_blocks(q):
    bsz, s = q.shape[0], q.shape[1]
    return q.reshape(bsz, s // Q_BLOCK, Q_BLOCK, *q.shape[2:]).swapaxes(0, 1)


def merge_query_blocks(o):
    nb, bsz, qb = o.shape[0], o.shape[1], o.shape[2]
    return o.swapaxes(0, 1).reshape(bsz, nb * qb, *o.shape[3:])


def gqa_attention(q, k, v):
    bsz, s, h, d = q.shape
    kvh = k.shape[2]
    qg = q.reshape(bsz, s, kvh, h // kvh, d)
    scale = d ** -0.5

    def block(qb):
        sc = jnp.einsum('bqhgd,bkhd->bhgqk', qb, k).astype(jnp.float32) * scale
        p = jax.nn.softmax(sc, axis=-1).astype(v.dtype)
        return jnp.einsum('bhgqk,bkhd->bqhgd', p, v)

    o = lax.map(block, split_query_blocks(qg))
    return merge_query_blocks(o).reshape(bsz, s, h * d)


def diff_attention(q, k, v, lam):
    scale = q.shape[-1] ** -0.5

    def block(qb):
        sc = jnp.einsum('bqhcd,bkhcd->cbhqk', qb, k).astype(jnp.float32) * scale
        p = jax.nn.softmax(sc, axis=-1)
        w = (p[0] - lam * p[1]).astype(v.dtype)
        return jnp.einsum('bhqk,bkhd->bqhd', w, v)

    o = lax.map(block, split_query_blocks(q))
    return merge_query_blocks(o)


def mixer_a_qkv(h, w_qkv, qn, kn):
    bsz, s, _ = h.shape
    qkv = h @ w_qkv
    q, k, v = jnp.split(qkv, [A_HEADS * A_HEAD_DIM, (A_HEADS + A_KV_HEADS) * A_HEAD_DIM], axis=-1)
    q = rms_norm(q.reshape(bsz, s, A_HEADS, A_HEAD_DIM), qn)
    k = rms_norm(k.reshape(bsz, s, A_KV_HEADS, A_HEAD_DIM), kn)
    v = v.reshape(bsz, s, A_KV_HEADS, A_HEAD_DIM)
    return q, k, v


def mixer_b_qkv(h, w_qkv):
    bsz, s, _ = h.shape
    qk_w = B_HEADS * 2 * B_QK_DIM
    q, k, v = jnp.split(h @ w_qkv, [qk_w, 2 * qk_w], axis=-1)
    q = q.reshape(bsz, s, B_HEADS, 2, B_QK_DIM)
    k = k.reshape(bsz, s, B_HEADS, 2, B_QK_DIM)
    v = v.reshape(bsz, s, B_HEADS, B_V_DIM)
    return q, k, v


def diff_lambda_init(layer_idx):
    return 0.8 - 0.6 * math.exp(-0.3 * layer_idx)


def diff_lambda(lam_p, lambda_init):
    lp = lam_p.astype(jnp.float32)
    return jnp.exp(jnp.sum(lp[0] * lp[1])) - jnp.exp(jnp.sum(lp[2] * lp[3])) + lambda_init


def diff_output(o, subln_g, lambda_init, w_o):
    bsz, s = o.shape[0], o.shape[1]
    o = rms_norm(o, subln_g) * (1.0 - lambda_init)
    return o.reshape(bsz, s, B_HEADS * B_V_DIM) @ w_o


def setup_inputs(seed: int = 0) -> dict:
    key = jax.random.key(seed)
    ks = jax.random.split(key, 24)
    f32 = jnp.float32
    nrm = lambda k, shp, sc: jax.random.normal(k, shp, f32) * sc
    qkv_a_w = (A_HEADS + 2 * A_KV_HEADS) * A_HEAD_DIM
    qkv_b_w = 2 * B_HEADS * 2 * B_QK_DIM + B_HEADS * B_V_DIM
    return {
        "x_prompt": nrm(ks[0], (BATCH, SEQ, D_MODEL), 1.0),
        "x_sample": nrm(ks[1], (DEC_BATCH, DEC_SEQ, D_MODEL), 1.0),
        "cache_a_k": nrm(ks[2], (DEC_BATCH, N_A_LAYERS, PAST_LEN, A_KV_HEADS, A_HEAD_DIM), 1.0),
        "cache_a_v": nrm(ks[3], (DEC_BATCH, N_A_LAYERS, PAST_LEN, A_KV_HEADS, A_HEAD_DIM), 1.0),
        "cache_b_k": nrm(ks[4], (DEC_BATCH, N_B_LAYERS, PAST_LEN, B_HEADS, 2, B_QK_DIM), 1.0),
        "cache_b_v": nrm(ks[5], (DEC_BATCH, N_B_LAYERS, PAST_LEN, B_HEADS, B_V_DIM), 1.0),
        "c": nrm(ks[6], (DEC_BATCH, D_MODEL), 1.0),
        "c_ctx": nrm(ks[7], (D_MODEL,), 1.0),
        "ada_w": nrm(ks[8], (DEPTH, D_MODEL, N_SUB * 3 * D_MODEL), 0.5 * D_MODEL ** -0.5),
        "ada_b": nrm(ks[9], (DEPTH, N_SUB * 3 * D_MODEL), 0.01),
        "ln_g": 1.0 + nrm(ks[10], (DEPTH, N_SUB, D_MODEL), 0.01),
        "ln_b": nrm(ks[11], (DEPTH, N_SUB, D_MODEL), 0.01),
        "ffn_w_in": nrm(ks[12], (DEPTH, 2, D_MODEL, 2 * D_FF), D_MODEL ** -0.5),
        "ffn_w_out": nrm(ks[13], (DEPTH, 2, D_FF, D_MODEL), DEEPNORM_BETA * D_FF ** -0.5),
        "a_w_qkv": nrm(ks[14], (N_A_LAYERS, D_MODEL, qkv_a_w), D_MODEL ** -0.5),
        "a_q_norm": 1.0 + nrm(ks[15], (N_A_LAYERS, A_HEAD_DIM), 0.01),
        "a_k_norm": 1.0 + nrm(ks[16], (N_A_LAYERS, A_HEAD_DIM), 0.01),
        "a_w_o": nrm(ks[17], (N_A_LAYERS, A_HEADS * A_HEAD_DIM, D_MODEL), DEEPNORM_BETA * (A_HEADS * A_HEAD_DIM) ** -0.5),
        "b_w_qkv": nrm(ks[18], (N_B_LAYERS, D_MODEL, qkv_b_w), D_MODEL ** -0.5),
        "b_lambda": nrm(ks[19], (N_B_LAYERS, 4, B_QK_DIM), 0.1),
        "b_subln": 1.0 + nrm(ks[20], (N_B_LAYERS, B_V_DIM), 0.01),
        "b_w_o": nrm(ks[21], (N_B_LAYERS, B_HEADS * B_V_DIM, D_MODEL), DEEPNORM_BETA * (B_HEADS * B_V_DIM) ** -0.5),
    }


def reference(x_prompt, x_sample, cache_a_k, cache_a_v, cache_b_k, cache_b_v, c, c_ctx,
              ada_w, ada_b, ln_g, ln_b, ffn_w_in, ffn_w_out,
              a_w_qkv, a_q_norm, a_k_norm, a_w_o,
              b_w_qkv, b_lambda, b_subln, b_w_o):
    x = x_prompt
    new_a_k, new_a_v, new_b_k, new_b_v = [], [], [], []
    for i in range(DEPTH):
        j = i // N_MIXERS
        m = modulation(c_ctx[None, :], ada_w[i], ada_b[i])
        x = ffn_sublayer(x, m, 0, ffn_w_in[i, 0], ffn_w_out[i, 0], ln_g[i, 0], ln_b[i, 0])
        h, gate = modulate(x, m, 1)
        if i % N_MIXERS == 0:
            q, k, v = mixer_a_qkv(h, a_w_qkv[j], a_q_norm[j], a_k_norm[j])
            y = gqa_attention(q, k, v) @ a_w_o[j]
            new_a_k.append(k)
            new_a_v.append(v)
        else:
            lam_init = diff_lambda_init(i)
            q, k, v = mixer_b_qkv(h, b_w_qkv[j])
            o = diff_attention(q, k, v, diff_lambda(b_lambda[j], lam_init))
            y = diff_output(o, b_subln[j], lam_init, b_w_o[j])
            new_b_k.append(k)
            new_b_v.append(v)
        x = post_norm(x, gate * y, ln_g[i, 1], ln_b[i, 1])
        x = ffn_sublayer(x, m, 2, ffn_w_in[i, 1], ffn_w_out[i, 1], ln_g[i, 2], ln_b[i, 2])
    y_prompt = x

    rows = x_sample.shape[1] // GRID_W
    cos_a, sin_a = axial_rope(rows, A_HEAD_DIM)
    cos_b, sin_b = axial_rope(rows, B_QK_DIM)
    x = x_sample
    for i in range(DEPTH):
        j = i // N_MIXERS
        m = modulation(c, ada_w[i], ada_b[i])
        x = ffn_sublayer(x, m, 0, ffn_w_in[i, 0], ffn_w_out[i, 0], ln_g[i, 0], ln_b[i, 0])
        h, gate = modulate(x, m, 1)
        if i % N_MIXERS == 0:
            q, k, v = mixer_a_qkv(h, a_w_qkv[j], a_q_norm[j], a_k_norm[j])
            q = apply_rope(q, cos_a, sin_a)
            k = apply_rope(k, cos_a, sin_a)
            k_all = jnp.concatenate([cache_a_k[:, j], k], axis=1)
            v_all = jnp.concatenate([cache_a_v[:, j], v], axis=1)
            y = gqa_attention(q, k_all, v_all) @ a_w_o[j]
        else:
            lam_init = diff_lambda_init(i)
            q, k, v = mixer_b_qkv(h, b_w_qkv[j])
            q = apply_rope(q, cos_b, sin_b)
            k = apply_rope(k, cos_b, sin_b)
            k_all = jnp.concatenate([cache_b_k[:, j], k], axis=1)
            v_all = jnp.concatenate([cache_b_v[:, j], v], axis=1)
            o = diff_attention(q, k_all, v_all, diff_lambda(b_lambda[j], lam_init))
            y = diff_output(o, b_subln[j], lam_init, b_w_o[j])
        x = post_norm(x, gate * y, ln_g[i, 1], ln_b[i, 1])
        x = ffn_sublayer(x, m, 2, ffn_w_in[i, 1], ffn_w_out[i, 1], ln_g[i, 2], ln_b[i, 2])
    y_sample = x

    new_a_k = jnp.stack(new_a_k, axis=1)
    new_a_v = jnp.stack(new_a_v, axis=1)
    new_b_k = jnp.stack(new_b_k, axis=1)
    new_b_v = jnp.stack(new_b_v, axis=1)
    return (y_prompt, y_sample, new_a_k, new_a_v, new_b_k, new_b_v)
```

```python
import math
import numpy as np
from contextlib import ExitStack
import concourse.bass as bass
import concourse.mybir as mybir
from concourse.bass_utils import run_bass_kernel_spmd

F32 = mybir.dt.float32
BF16 = mybir.dt.bfloat16
AF = mybir.ActivationFunctionType
ALU = mybir.AluOpType
AX = mybir.AxisListType

NCORES = 8
D = 2048
KC = 16
T = 512
DFF = 5504
NJ = 43
ALPHA = 4.0 ** 0.25
EPS_LN = 1e-6 / (ALPHA * ALPHA)
EPS = 1e-6
LAM_INIT = 0.8 - 0.6 * math.exp(-0.3 * 1)
NSLOT = 3
SLOT_E = 8192
BIG_E = 24576
ENGS = ("tensor", "vector", "scalar", "gpsimd", "sync")
RG = [[0, 1, 2, 3], [4, 5, 6, 7]]
import os
STOP_AFTER = os.environ.get("KSTOP", "")
KFAKE = bool(os.environ.get("KFAKE", ""))


class _Stop(Exception):
    pass


def ckpt(name):
    if STOP_AFTER == name:
        raise _Stop()


KCUT = int(os.environ.get("KCUT", "0"))
KH = int(os.environ.get("KH", "0"))
KQP = int(os.environ.get("KQP", "3"))


def cut(n, cond=True):
    if KCUT == n and cond:
        raise _Stop()


class Buf:
    def __init__(self, name):
        self.name = name
        self.w = None
        self.r = []


class Prog:
    def __init__(self):
        self.ops = {e: [] for e in ENGS}
        self.cnt = {}
        self.waited = {e: {} for e in ENGS}
        self.semkeys = []

    def _sem(self, key):
        if key not in self.cnt:
            self.cnt[key] = 0
            self.semkeys.append(key)
        return key

    def _deps(self, eng, reads, writes):
        evs = []
        for b in reads:
            if b.w is not None:
                evs.append(b.w)
        for b in writes:
            if b.w is not None:
                evs.append(b.w)
            evs.extend(b.r)
        own = "e_" + eng
        for (k, v) in evs:
            if k == own:
                continue
            if self.waited[eng].get(k, 0) < v:
                self.waited[eng][k] = v
                self.ops[eng].append(("wait", k, v))

    def op(self, eng, fns, reads=(), writes=()):
        if not isinstance(fns, (list, tuple)):
            fns = [fns]
        self._deps(eng, reads, writes)
        key = self._sem("e_" + eng)
        self.cnt[key] += 1
        ev = (key, self.cnt[key])
        for f in fns[:-1]:
            self.ops[eng].append(("op", f, None, 0))
        self.ops[eng].append(("op", fns[-1], key, 1))
        for b in writes:
            b.w = ev
            b.r = []
        for b in reads:
            b.r.append(ev)
        return ev

    def dma(self, eng, fns, reads, writes):
        if not isinstance(fns, (list, tuple)):
            fns = [fns]
        self._deps(eng, reads, writes)
        key = self._sem("d_" + writes[0].name)
        for f in fns:
            self.cnt[key] += 16
            self.ops[eng].append(("op", f, key, 16))
        ev = (key, self.cnt[key])
        for b in writes:
            b.w = ev
            b.r = []
        for b in reads:
            b.r.append(ev)
        return ev

    def wait_all(self, eng, bufs):
        self._deps(eng, bufs, ())

    def drain(self, eng):
        key = "e_" + eng
        v = self.cnt.get(key, 0)
        if v > 0:
            self.ops[eng].append(("wait", key, v))

    def emit(self, nc, stack):
        sems = {}
        for k in self.semkeys:
            sems[k] = stack.enter_context(nc.semaphore(k))
        block = stack.enter_context(nc.Block())

        def replay(name):
            def run(e):
                for item in self.ops[name]:
                    if item[0] == "wait":
                        e.wait_ge(sems[item[1]], item[2])
                    else:
                        ins = item[1](e)
                        if item[2] is not None:
                            ins.then_inc(sems[item[2]], item[3])
            return run

        for name in ENGS:
            if self.ops[name]:
                getattr(block, name)(replay(name))


class WStream:
    def __init__(self, P, slot_bufs, loaders=None):
        self.P = P
        self.bufs = slot_bufs
        self.dry = loaders is None
        self.loaders = [] if self.dry else loaders
        self.i = 0
        self.issued = 0

    def _issue(self, upto):
        while self.issued < min(upto, len(self.loaders)):
            n = self.issued
            s = n % NSLOT
            self.P.dma("gpsimd", self.loaders[n](s), [], [self.bufs[s]])
            self.issued += 1

    def next(self, loader):
        if self.dry:
            self.loaders.append(loader)
            return 0, self.bufs[0]
        i = self.i
        self._issue(i + NSLOT)
        self.i += 1
        return i % NSLOT, self.bufs[i % NSLOT]


def build_nc():
    nc = bass.Bass("TRN2", target_bir_lowering=False)

    def din(name, shape):
        if KFAKE and name in ("w_in", "w_out", "a_qkv", "a_wo", "b_qkv", "b_wo", "adaw"):
            return nc.dram_tensor(name, list(shape), F32).ap()
        return nc.dram_tensor(name, list(shape), F32, kind="ExternalInput").ap()

    def dout(name, shape):
        return nc.dram_tensor(name, list(shape), F32, kind="ExternalOutput").ap()

    xp_d = din("xp", [T, D])
    xs_d = din("xs", [T, D])
    cak_d = din("cak", [256, 512])
    cav_d = din("cav", [256, 512])
    cbk_d = din("cbk", [256, 2048])
    cbv_d = din("cbv", [256, 2048])
    cond_d = din("condT", [128, 32])
    adaw_d = din("adaw", [2, D, 4608])
    adab_d = din("adab", [128, 72])
    lnT_d = din("lnT", [128, 192])
    if KFAKE:
        class _W4:
            def __init__(self):
                self.t = {(l, f): nc.dram_tensor("w_in%d%d" % (l, f), [D, 2 * DFF], F32).ap() for l in range(2) for f in range(2)}

            def __getitem__(self, k):
                return self.t[k]
        win_d = _W4()
    else:
        win_d = din("w_in", [2, 2, D, 2 * DFF])
    wout_d = din("w_out", [2, 2, DFF, D])
    aqkv_d = din("a_qkv", [D, 3072])
    awo_d = din("a_wo", [D, D])
    bqkv_d = din("b_qkv", [D, 6144])
    bwo_d = din("b_wo", [D, D])
    gains_d = din("gains", [128, 3])
    blam_d = din("blam", [128, 256])
    rope_d = din("rope", [128, 4 * T])
    rot_d = din("rot", [128, 256])

    yp_d = dout("yp", [T, D])
    ys_d = dout("ys", [T, D])
    nak_d = dout("nak", [T, 512])
    nav_d = dout("nav", [T, 512])
    nbk_d = dout("nbk", [T, 2048])
    nbv_d = dout("nbv", [T, 2048])

    mod_in = nc.dram_tensor("mod_in", [128, 144], F32)
    mod_out = nc.dram_tensor("mod_out", [512, 144], F32)
    NSP = {"A": 1, "B": 4}
    kx = {}
    for kd, kw in (("A", 1024), ("B", 4096)):
        for part in ("K", "V"):
            for sp in range(NSP[kd]):
                kx[(kd, part, "in", sp)] = nc.dram_tensor("kx%s%s_in%d" % (kd, part, sp), [128, kw // NSP[kd]], F32)
                kx[(kd, part, "all", sp)] = nc.dram_tensor("kx%s%s_all%d" % (kd, part, sp), [512, kw // NSP[kd]], F32)

    with ExitStack() as st:
        def sb(name, shape, dt):
            return st.enter_context(nc.sbuf_tensor(name, list(shape), dt))

        def ps(name, shape, dt):
            return st.enter_context(nc.psum_tensor(name, list(shape), dt))

        xT = sb("xT", [128, KC, T], F32)
        hT = sb("hT", [128, KC, T], BF16)
        big = sb("big", [128, BIG_E], BF16)
        bigf = big.bitcast(F32)
        slots = [sb("slot%d" % i, [128, SLOT_E], BF16) for i in range(NSLOT)]
        kTh = [sb("kTh%d" % i, [128, 2048], BF16) for i in range(2)]
        kThf = [t.bitcast(F32) for t in kTh]
        stage = sb("stage", [128, D], F32)
        NSCR = 6
        scr = [sb("scr%d" % i, [128, T], F32) for i in range(NSCR)]
        ebuf = [sb("eb%d" % i, [128, T], BF16) for i in range(4)]
        modR = sb("modR", [128, 576], F32)
        modS = sb("modS", [128, 576], F32)
        modG1 = sb("modG1", [128, 576], F32)
        modG5 = sb("modG5", [128, 576], F32)
        ropet = sb("ropet", [128, 4 * T], F32)
        ident = sb("ident", [128, 128], F32)
        ones32 = sb("ones32", [128, 128], F32)
        onesbf = sb("onesbf", [128, 128], BF16)
        rott = sb("rott", [128, 256], F32)
        condt = sb("condt", [128, 32], F32)
        scT = sb("scT", [128, 32], BF16)
        lnT = sb("lnT_t", [128, 192], F32)
        adab = sb("adab_t", [128, 72], F32)
        mpart = sb("mpart", [128, 144], F32)
        gains = sb("gains_t", [128, 3], F32)
        lamt = sb("lamt", [128, 256], F32)
        small = sb("small", [128, 16], F32)
        pb = [ps("pb%d" % i, [128, T], F32) for i in range(8)]

        b_x = [Buf("x%d" % k) for k in range(KC)]
        b_h = Buf("hT")
        b_act = [Buf("act%d" % j) for j in range(NJ)]
        b_q, b_kown, b_vown = Buf("q"), Buf("kown"), Buf("vown")
        b_slot = [Buf("slot%d" % i) for i in range(NSLOT)]
        b_kth = [Buf("kth%d" % i) for i in range(2)]
        b_vh = [Buf("vh%d" % i) for i in range(2)]
        b_stage = Buf("stage")
        b_scr = [Buf("scr%d" % i) for i in range(NSCR)]
        b_eb = [Buf("eb%d" % i) for i in range(4)]
        b_qp = [Buf("qp%d" % i) for i in range(2)]
        b_mod, b_rope, b_const, b_small = Buf("mod"), Buf("rope"), Buf("const"), Buf("small")
        b_cond, b_scT, b_mpart = Buf("cond"), Buf("scT"), Buf("mpart")
        b_pb = [Buf("pb%d" % i) for i in range(8)]
        b_modin, b_modout = Buf("modin"), Buf("modout")
        b_kvin, b_kvall = {"A": Buf("kvAin"), "B": Buf("kvBin")}, {"A": Buf("kvAall"), "B": Buf("kvBall")}
        b_out = {n: Buf("o_" + n) for n in ["yp", "ys", "nak", "nav", "nbk", "nbv"]}

        def bigv(off, a, b):
            return bass.AP(big, off, [[BIG_E, 128], [b, a], [1, b]])

        def bigfv(off_e, n):
            return bass.AP(bigf, off_e // 2, [[BIG_E // 2, 128], [1, n // 2]])

        def sv(s, a, b):
            return bass.AP(slots[s], 0, [[SLOT_E, 128], [b, a], [1, b]])

        def pbv(i, a, b):
            return bass.AP(pb[i], 0, [[T, 128], [b, a], [1, b]])

        def scrv(i, a, b):
            return bass.AP(scr[i], 0, [[T, 128], [b, a], [1, b]])

        actT = bigv(0, NJ, T)
        qT = bigv(0, 16, T)
        kTown = bigv(8192, 16, T)
        OFF_K, OFF_V = 8192, 16384

        def mcol(tile, s, t, k, l, c):
            g = (s * 3 + t) * 16 + k
            col = (g * 2 + l) * 2 + c
            return tile[:, col:col + 1]

        def lncol(gb, l, s, k):
            col = (gb * 6 + l * 3 + s) * 16 + k
            return lnT[:, col:col + 1]

        eps_ln_col = small[:, 0:1]
        eps_col = small[:, 1:2]
        neglam_col = small[:, 2:3]
        subg_col = small[:, 3:4]

        def program(P, ws):
            sc_i = [0]

            def S():
                i = sc_i[0] % NSCR
                sc_i[0] += 1
                return i

            eb_i = [0]

            def E():
                i = eb_i[0] % 4
                eb_i[0] += 1
                return i

            P.op("vector", [lambda e: e.memset(ones32[:], 1.0),
                            lambda e: e.memset(onesbf[:], 1.0),
                            lambda e: e.memset(small[:, 0:1], EPS_LN),
                            lambda e: e.memset(small[:, 1:2], EPS)], [], [b_const, b_small])
            P.op("gpsimd", [lambda e: e.memset(ident[:], 0.0),
                            lambda e: e.affine_select(out=ident[:], in_=ident[:], pattern=[[-1, 128]],
                                                      compare_op=ALU.not_equal, fill=1.0, base=0,
                                                      channel_multiplier=1)], [], [b_const])
            P.dma("sync", [lambda e: e.dma_start(out=condt[:], in_=cond_d),
                           lambda e: e.dma_start(out=lnT[:], in_=lnT_d),
                           lambda e: e.dma_start(out=adab[:], in_=adab_d),
                           lambda e: e.dma_start(out=gains[:], in_=gains_d),
                           lambda e: e.dma_start(out=lamt[:], in_=blam_d),
                           lambda e: e.dma_start(out=rott[:], in_=rot_d),
                           lambda e: e.dma_start(out=ropet[:], in_=rope_d)], [], [b_cond])
            P.op("vector", [lambda e: e.tensor_tensor(out=scr[0][:, 0:64], in0=lamt[:, 0:64], in1=lamt[:, 64:128], op=ALU.mult),
                            lambda e: e.tensor_tensor(out=scr[0][:, 64:128], in0=lamt[:, 128:192], in1=lamt[:, 192:256], op=ALU.mult)],
                 [b_cond], [b_scr[0]])
            P.drain("vector")
            P.op("vector", lambda e: e.reduce_sum(out=small[:, 4:5], in_=scr[0][:, 0:64], axis=AX.X), [b_scr[0]], [b_small])
            P.op("vector", lambda e: e.reduce_sum(out=small[:, 5:6], in_=scr[0][:, 64:128], axis=AX.X), [b_scr[0]], [b_small])
            P.drain("vector")
            P.op("scalar", lambda e: e.activation(out=small[:, 6:8], in_=small[:, 4:6], func=AF.Exp), [b_small], [b_small])
            P.op("vector", lambda e: e.tensor_tensor(out=small[:, 8:9], in0=small[:, 7:8], in1=small[:, 6:7], op=ALU.subtract),
                 [b_small], [b_small])
            P.drain("vector")
            P.op("vector", lambda e: e.tensor_scalar(out=small[:, 2:3], in0=small[:, 8:9], scalar1=-LAM_INIT, scalar2=None, op0=ALU.add),
                 [b_small], [b_small])
            P.op("vector", lambda e: e.tensor_scalar(out=small[:, 3:4], in0=gains[:, 2:3], scalar1=1.0 - LAM_INIT, scalar2=None, op0=ALU.mult),
                 [b_small, b_cond], [b_small])
            P.drain("vector")

            ckpt("const")
            P.op("scalar", lambda e: e.activation(out=scT[:], in_=condt[:], func=AF.Silu), [b_cond], [b_scT])
            scT3 = bass.AP(scT, 0, [[32, 128], [2, 16], [1, 2]])
            for l in range(2):
                for wt in range(9):
                    def loader(s, l=l, wt=wt):
                        return [lambda e: e.dma_start(out=sv(s, 16, 512),
                                                      in_=adaw_d[l][:, wt * 512:(wt + 1) * 512].rearrange("(k p) n -> p k n", p=128))]
                    s, sbuf = ws.next(loader)
                    wv = sv(s, 16, 512)
                    for c4 in range(4):
                        cc = wt * 4 + c4
                        bk = cc % 2
                        P.op("tensor", [lambda e, kc=kc, wv=wv, c4=c4, bk=bk: e.matmul(
                            pb[bk][:, 0:2], lhsT=wv[:, kc, c4 * 128:(c4 + 1) * 128], rhs=scT3[:, kc, :],
                            start=(kc == 0), stop=(kc == KC - 1)) for kc in range(KC)], [sbuf, b_scT], [b_pb[bk]])
                        idx = (cc * 2 + l) * 2
                        P.op("vector", lambda e, bk=bk, idx=idx, cc=cc, l=l: e.tensor_scalar(
                            out=mpart[:, idx:idx + 2], in0=pb[bk][:, 0:2], scalar1=adab[:, cc * 2 + l:cc * 2 + l + 1],
                            scalar2=None, op0=ALU.add), [b_pb[bk], b_cond], [b_mpart])
            P.dma("sync", lambda e: e.dma_start(out=mod_in.ap(), in_=mpart[:]), [b_mpart], [b_modin])
            P.op("gpsimd", lambda e: e.collective_compute("AllGather", ALU.bypass, replica_groups=RG,
                                                          ins=[mod_in.ap()], outs=[mod_out.ap()]), [b_modin], [b_modout])
            P.dma("sync", lambda e: e.dma_start(out=bass.AP(modR, 0, [[576, 128], [144, 4], [1, 144]]),
                                                in_=mod_out.ap().rearrange("(r p) n -> p r n", p=128)), [b_modout], [b_mod])
            P.op("vector", [lambda e: e.tensor_scalar(out=modS[:], in0=modR[:], scalar1=1.0, scalar2=None, op0=ALU.add),
                            lambda e: e.tensor_scalar(out=modG1[:], in0=modR[:], scalar1=1.0 / ALPHA, scalar2=None, op0=ALU.mult),
                            lambda e: e.tensor_scalar(out=modG5[:], in0=modR[:], scalar1=0.5 / ALPHA, scalar2=None, op0=ALU.mult)],
                 [b_mod], [b_mod])

            ckpt("mod")
            def load_x(src):
                for t in range(4):
                    P.dma("sync", lambda e, t=t: e.dma_start(out=stage[:], in_=src[t * 128:(t + 1) * 128, :]), [], [b_stage])
                    for k4 in range(4):
                        bk = (t * 4 + k4) % 2
                        P.op("tensor", [lambda e, c=c, k4=k4, bk=bk: e.transpose(
                            pb[bk][:, c * 128:(c + 1) * 128], stage[:, (k4 * 4 + c) * 128:(k4 * 4 + c + 1) * 128], ident[:])
                            for c in range(4)], [b_stage, b_const], [b_pb[bk]])
                        P.op("vector", lambda e, t=t, k4=k4, bk=bk: e.tensor_copy(
                            out=xT[:, k4 * 4:(k4 + 1) * 4, t * 128:(t + 1) * 128], in_=pbv(bk, 4, 128)),
                            [b_pb[bk]], b_x[k4 * 4:(k4 + 1) * 4])

            def store_x(dst, ob):
                for t in range(4):
                    for k4 in range(4):
                        bk = (t * 4 + k4) % 2
                        P.op("tensor", [lambda e, c=c, k4=k4, bk=bk, t=t: e.transpose(
                            pb[bk][:, c * 128:(c + 1) * 128], xT[:, k4 * 4 + c, t * 128:(t + 1) * 128], ident[:])
                            for c in range(4)], b_x[k4 * 4:(k4 + 1) * 4] + [b_const], [b_pb[bk]])
                        P.op("vector", lambda e, k4=k4, bk=bk: e.tensor_copy(
                            out=stage[:, k4 * 512:(k4 + 1) * 512], in_=pb[bk][:]), [b_pb[bk]], [b_stage])
                    P.dma("sync", lambda e, t=t: e.dma_start(out=dst[t * 128:(t + 1) * 128, :], in_=stage[:]), [b_stage], [ob])

            def modulate(l, ci, s):
                for k in range(KC):
                    P.op("scalar", lambda e, k=k: e.activation(
                        out=hT[:, k, :], in_=xT[:, k, :], func=AF.Identity,
                        scale=mcol(modS, s, 1, k, l, ci), bias=mcol(modR, s, 0, k, l, ci)),
                        [b_x[k], b_mod], [b_h])

            def layernorm(l, s):
                for fc in range(KC):
                    zi = S()
                    P.op("scalar", lambda e, fc=fc, zi=zi: e.activation(out=scr[zi][:], in_=xT[:, fc, :], func=AF.Square),
                         [b_x[fc]], [b_scr[zi]])
                    P.op("tensor", lambda e, fc=fc: e.matmul(pb[6][:], lhsT=ones32[:], rhs=xT[:, fc, :],
                                                             start=(fc == 0), stop=(fc == KC - 1)), [b_x[fc], b_const], [b_pb[6]])
                    P.op("tensor", lambda e, fc=fc, zi=zi: e.matmul(pb[7][:], lhsT=ones32[:], rhs=scr[zi][:],
                                                                    start=(fc == 0), stop=(fc == KC - 1)), [b_scr[zi], b_const], [b_pb[7]])
                mi, vi, ni = S(), S(), S()
                P.op("vector", lambda e: e.tensor_scalar(out=scr[mi][:], in0=pb[6][:], scalar1=1.0 / D, scalar2=None, op0=ALU.mult),
                     [b_pb[6]], [b_scr[mi]])
                P.op("vector", lambda e: e.tensor_tensor(out=scr[ni][:], in0=scr[mi][:], in1=scr[mi][:], op=ALU.mult),
                     [b_scr[mi]], [b_scr[ni]])
                P.op("vector", lambda e: e.scalar_tensor_tensor(out=scr[vi][:], in0=pb[7][:], scalar=1.0 / D, in1=scr[ni][:],
                                                                op0=ALU.mult, op1=ALU.subtract), [b_pb[7], b_scr[ni]], [b_scr[vi]])
                P.op("scalar", lambda e: e.activation(out=scr[vi][:], in_=scr[vi][:], func=AF.Sqrt, bias=eps_ln_col, scale=1.0),
                     [b_scr[vi], b_small], [b_scr[vi]])
                P.op("vector", lambda e: e.reciprocal(out=scr[vi][:], in_=scr[vi][:]), [b_scr[vi]], [b_scr[vi]])
                P.op("vector", lambda e: e.scalar_tensor_tensor(out=scr[ni][:], in0=scr[mi][:], scalar=-1.0, in1=scr[vi][:],
                                                                op0=ALU.mult, op1=ALU.mult), [b_scr[mi], b_scr[vi]], [b_scr[ni]])
                for fc in range(KC):
                    P.op("vector", [lambda e, fc=fc: e.tensor_tensor(out=xT[:, fc, :], in0=xT[:, fc, :], in1=scr[vi][:], op=ALU.mult),
                                    lambda e, fc=fc: e.tensor_tensor(out=xT[:, fc, :], in0=xT[:, fc, :], in1=scr[ni][:], op=ALU.add)],
                         [b_x[fc], b_scr[vi], b_scr[ni]], [b_x[fc]])
                    P.op("scalar", lambda e, fc=fc: e.activation(out=xT[:, fc, :], in_=xT[:, fc, :], func=AF.Identity,
                                                                 scale=lncol(0, l, s, fc), bias=lncol(1, l, s, fc)),
                         [b_x[fc], b_cond], [b_x[fc]])

            def ffn(l, f, ci, s):
                modulate(l, ci, s)
                w_in = win_d[l, f]
                w_out = wout_d[l, f]
                for j2 in range(22):
                    nj = 2 if j2 < 21 else 1

                    def loader(sl, j2=j2, nj=nj):
                        c0 = j2 * 256
                        wv = sv(sl, 16, 512)
                        return [lambda e: e.dma_start(out=wv[:, :, 0:nj * 128],
                                                      in_=w_in[:, c0:c0 + nj * 128].rearrange("(k p) n -> p k n", p=128)),
                                lambda e: e.dma_start(out=wv[:, :, 256:256 + nj * 128],
                                                      in_=w_in[:, DFF + c0:DFF + c0 + nj * 128].rearrange("(k p) n -> p k n", p=128))]
                    sl, sbuf = ws.next(loader)
                    wv = sv(sl, 16, 512)
                    for jj in range(nj):
                        j = j2 * 2 + jj
                        ga, ua = (j % 2) * 2, (j % 2) * 2 + 1
                        P.op("tensor", [lambda e, kc=kc, wv=wv, jj=jj, ga=ga: e.matmul(
                            pb[ga][:], lhsT=wv[:, kc, jj * 128:(jj + 1) * 128], rhs=hT[:, kc, :],
                            start=(kc == 0), stop=(kc == KC - 1)) for kc in range(KC)], [sbuf, b_h], [b_pb[ga]])
                        P.op("tensor", [lambda e, kc=kc, wv=wv, jj=jj, ua=ua: e.matmul(
                            pb[ua][:], lhsT=wv[:, kc, 256 + jj * 128:256 + (jj + 1) * 128], rhs=hT[:, kc, :],
                            start=(kc == 0), stop=(kc == KC - 1)) for kc in range(KC)], [sbuf, b_h], [b_pb[ua]])
                        si = S()
                        P.op("scalar", lambda e, si=si, ga=ga: e.activation(out=scr[si][:], in_=pb[ga][:], func=AF.Silu),
                             [b_pb[ga]], [b_scr[si]])
                        P.op("vector", lambda e, si=si, ua=ua, j=j: e.tensor_tensor(
                            out=actT[:, j, :], in0=pb[ua][:], in1=scr[si][:], op=ALU.mult),
                            [b_pb[ua], b_scr[si]], [b_act[j]])
                for cb in range(8):
                    for half in range(2):
                        j0, njh = (0, 22) if half == 0 else (22, 21)

                        def loader(sl, cb=cb, j0=j0, njh=njh):
                            return [lambda e: e.dma_start(
                                out=sv(sl, 22, 256)[:, 0:njh, :],
                                in_=w_out[j0 * 128:(j0 + njh) * 128, cb * 256:(cb + 1) * 256].rearrange("(j p) n -> p j n", p=128))]
                        sl, sbuf = ws.next(loader)
                        wv = sv(sl, 22, 256)
                        for f2 in range(2):
                            yb = 4 + f2
                            P.op("tensor", [lambda e, jj=jj, wv=wv, yb=yb, f2=f2, j0=j0, half=half, njh=njh: e.matmul(
                                pb[yb][:], lhsT=wv[:, jj, f2 * 128:(f2 + 1) * 128], rhs=actT[:, j0 + jj, :],
                                start=(half == 0 and jj == 0), stop=(half == 1 and jj == njh - 1))
                                for jj in range(njh)], [sbuf] + b_act[j0:j0 + njh], [b_pb[yb]])
                    for f2 in range(2):
                        fc = cb * 2 + f2
                        yb = 4 + f2
                        P.op("vector", lambda e, fc=fc, yb=yb: e.scalar_tensor_tensor(
                            out=xT[:, fc, :], in0=pb[yb][:], scalar=mcol(modG5, s, 2, fc, l, ci), in1=xT[:, fc, :],
                            op0=ALU.mult, op1=ALU.add), [b_pb[yb], b_x[fc], b_mod], [b_x[fc]])
                layernorm(l, s)

            def mixer(l, ci, sample, kind):
                s = 1
                modulate(l, ci, s)
                if kind == "A":
                    NK, NVB, wqkv, vcol0, wo = 4, 1, aqkv_d, 2560, awo_d
                    ck_d, cv_d, nk_d, nv_d, ob_k, ob_v = cak_d, cav_d, nak_d, nav_d, b_out["nak"], b_out["nav"]
                    sm_scale = 128.0 ** -0.5
                    rT, cosT, sinT = rott[:, 0:128], ropet[:, 0:T], ropet[:, T:2 * T]
                else:
                    NK, NVB, wqkv, vcol0, wo = 16, 4, bqkv_d, 4096, bwo_d
                    ck_d, cv_d, nk_d, nv_d, ob_k, ob_v = cbk_d, cbv_d, nbk_d, nbv_d, b_out["nbk"], b_out["nbv"]
                    sm_scale = 64.0 ** -0.5
                    rT, cosT, sinT = rott[:, 128:256], ropet[:, 2 * T:3 * T], ropet[:, 3 * T:4 * T]
                VW = NK * 128
                Vown = bigv(OFF_V, 4, VW)
                NQ = 16
                for t4 in range((NQ + NK) // 4):
                    def loader(sl, t4=t4):
                        return [lambda e: e.dma_start(out=sv(sl, 16, 512),
                                                      in_=wqkv[:, t4 * 512:(t4 + 1) * 512].rearrange("(k p) n -> p k n", p=128))]
                    sl, sbuf = ws.next(loader)
                    wv = sv(sl, 16, 512)
                    for c4 in range(4):
                        c = t4 * 4 + c4
                        isq = c < NQ
                        h = c if isq else c - NQ
                        rb = c % 2
                        dst = qT[:, h, :] if isq else kTown[:, h, :]
                        dbuf = b_q if isq else b_kown
                        P.op("tensor", [lambda e, kc=kc, wv=wv, c4=c4, rb=rb: e.matmul(
                            pb[rb][:], lhsT=wv[:, kc, c4 * 128:(c4 + 1) * 128], rhs=hT[:, kc, :],
                            start=(kc == 0), stop=(kc == KC - 1)) for kc in range(KC)], [sbuf, b_h], [b_pb[rb]])
                        cut(1)
                        if kind == "A":
                            qi, si, ri = S(), S(), S()
                            P.op("scalar", [lambda e, si=si, rb=rb: e.activation(out=scr[si][:], in_=pb[rb][:], func=AF.Square),
                                            lambda e, qi=qi, rb=rb: e.activation(out=scr[qi][:], in_=pb[rb][:], func=AF.Copy)],
                                 [b_pb[rb]], [b_scr[si], b_scr[qi]])
                            cut(2)
                            P.op("tensor", lambda e, si=si, rb=rb: e.matmul(pb[2 + rb][:], lhsT=ones32[:], rhs=scr[si][:],
                                                                            start=True, stop=True), [b_scr[si], b_const], [b_pb[2 + rb]])
                            cut(3)
                            P.op("scalar", lambda e, ri=ri, rb=rb: e.activation(out=scr[ri][:], in_=pb[2 + rb][:], func=AF.Sqrt,
                                                                                bias=eps_col, scale=1.0 / 128), [b_pb[2 + rb], b_small], [b_scr[ri]])
                            cut(4)
                            P.op("vector", [lambda e, ri=ri: e.reciprocal(out=scr[ri][:], in_=scr[ri][:]),
                                            lambda e, ri=ri, qi=qi: e.tensor_tensor(out=scr[qi][:], in0=scr[qi][:], in1=scr[ri][:], op=ALU.mult)],
                                 [b_scr[ri], b_scr[qi]], [b_scr[ri], b_scr[qi]])
                            cut(5)
                            gcol = gains[:, 0:1] if isq else gains[:, 1:2]
                            src, srcb = scr[qi][:], b_scr[qi]
                        else:
                            gcol = 1.0
                            src, srcb = pb[rb][:], b_pb[rb]
                        if sample:
                            xi, t1, t2 = S(), S(), S()
                            P.op("scalar", lambda e, xi=xi, src=src, gcol=gcol: e.activation(
                                out=scr[xi][:], in_=src, func=AF.Identity, scale=gcol), [srcb, b_cond], [b_scr[xi]])
                            P.op("tensor", lambda e, xi=xi, rb=rb: e.matmul(pb[4 + rb][:], lhsT=rT, rhs=scr[xi][:], start=True, stop=True),
                                 [b_scr[xi], b_cond], [b_pb[4 + rb]])
                            P.op("vector", lambda e, xi=xi, t1=t1: e.tensor_tensor(out=scr[t1][:], in0=scr[xi][:], in1=cosT, op=ALU.mult),
                                 [b_scr[xi], b_cond], [b_scr[t1]])
                            P.op("vector", lambda e, t2=t2, rb=rb: e.tensor_tensor(out=scr[t2][:], in0=pb[4 + rb][:], in1=sinT, op=ALU.mult),
                                 [b_pb[4 + rb], b_cond], [b_scr[t2]])
                            P.op("vector", lambda e, t1=t1, t2=t2, dst=dst: e.tensor_tensor(out=dst, in0=scr[t1][:], in1=scr[t2][:], op=ALU.add),
                                 [b_scr[t1], b_scr[t2]], [dbuf])
                        else:
                            P.op("scalar", lambda e, src=src, gcol=gcol, dst=dst: e.activation(
                                out=dst, in_=src, func=AF.Identity, scale=gcol), [srcb, b_cond], [dbuf])
                            cut(6)
                            if not isq:
                                ki, ko = S(), S()
                                P.op("scalar", lambda e, ki=ki, src=src, gcol=gcol: e.activation(
                                    out=scr[ki][:], in_=src, func=AF.Identity, scale=gcol), [srcb, b_cond], [b_scr[ki]])
                                cut(7)
                                P.op("tensor", [lambda e, t=t, ki=ki, rb=rb: e.transpose(
                                    pb[6 + rb][:, t * 128:(t + 1) * 128], scr[ki][:, t * 128:(t + 1) * 128], ident[:])
                                    for t in range(4)], [b_scr[ki], b_const], [b_pb[6 + rb]])
                                cut(8)
                                P.op("vector", lambda e, ko=ko, rb=rb: e.tensor_copy(out=scr[ko][:], in_=pb[6 + rb][:]),
                                     [b_pb[6 + rb]], [b_scr[ko]])
                                cut(9)
                                P.dma("sync", lambda e, ko=ko, h=h: e.dma_start(
                                    out=nk_d[:, h * 128:(h + 1) * 128].rearrange("(t p) d -> p t d", p=128), in_=scrv(ko, 4, 128)),
                                    [b_scr[ko]], [ob_k])
                                cut(10)
                        cut(11, c == 17)
                        cut(12, c == 18)
                        cut(13, c == 19)
                mtag = ("s" if sample else "p") + kind
                ckpt(mtag + "_qk")
                for vb in range(NVB):
                    def loader(sl, vb=vb):
                        return [lambda e: e.dma_start(out=sv(sl, 16, 512),
                                                      in_=wqkv[:, vcol0 + vb * 512:vcol0 + (vb + 1) * 512].rearrange("(k p) n -> p k n", p=128))]
                    sl, sbuf = ws.next(loader)
                    wv = sv(sl, 16, 512)
                    for t in range(4):
                        vbk = 6 + t % 2
                        P.op("tensor", [lambda e, kc=kc, wv=wv, t=t, vbk=vbk: e.matmul(
                            pb[vbk][:], lhsT=hT[:, kc, t * 128:(t + 1) * 128], rhs=wv[:, kc, :],
                            start=(kc == 0), stop=(kc == KC - 1)) for kc in range(KC)], [sbuf, b_h], [b_pb[vbk]])
                        vi = S()
                        P.op("vector", lambda e, vi=vi, vbk=vbk: e.tensor_copy(out=scr[vi][:], in_=pb[vbk][:]),
                             [b_pb[vbk]], [b_scr[vi]])
                        P.op("scalar", lambda e, t=t, vb=vb, vi=vi: e.activation(
                            out=Vown[:, t, vb * 512:(vb + 1) * 512], in_=scr[vi][:], func=AF.Copy), [b_scr[vi]], [b_vown])
                        if not sample:
                            P.dma("sync", lambda e, vi=vi, t=t, vb=vb: e.dma_start(
                                out=nv_d[t * 128:(t + 1) * 128, vb * 512:(vb + 1) * 512], in_=scr[vi][:]), [b_scr[vi]], [ob_v])
                ckpt(mtag + "_v")
                KW = NK * 256
                if sample:
                    nsp = NSP[kind]
                    HPS, TPS = NK // nsp, 4 // nsp
                    P.dma("sync", [lambda e, sp=sp: e.dma_start(out=kx[(kind, "K", "in", sp)].ap(),
                                                                 in_=bigfv(OFF_K + sp * HPS * T, HPS * T)) for sp in range(nsp)],
                          [b_kown], [b_kvin[kind]])
                    P.dma("sync", [lambda e, sp=sp: e.dma_start(out=kx[(kind, "V", "in", sp)].ap(),
                                                                 in_=bigfv(OFF_V + sp * TPS * VW, TPS * VW)) for sp in range(nsp)],
                          [b_vown, b_kvin[kind]], [b_kvin[kind]])
                    for part in ("K", "V"):
                        for sp in range(nsp):
                            P.op("gpsimd", lambda e, part=part, sp=sp: e.collective_compute(
                                "AllGather", ALU.bypass, replica_groups=RG,
                                ins=[kx[(kind, part, "in", sp)].ap()], outs=[kx[(kind, part, "all", sp)].ap()]),
                                [b_kvin[kind]], [b_kvall[kind]])
                    cut(41, kind == "B")
                    cKT = bigv(OFF_K, NK, 256)
                    for tc in range(2):
                        P.dma("sync", lambda e, tc=tc: e.dma_start(out=stage[:, 0:VW], in_=ck_d[tc * 128:(tc + 1) * 128, :]), [], [b_stage])
                        for h4 in range(NK // 4):
                            bk = h4 % 2
                            P.op("tensor", [lambda e, c=c, h4=h4, bk=bk: e.transpose(
                                pb[bk][:, c * 128:(c + 1) * 128], stage[:, (h4 * 4 + c) * 128:(h4 * 4 + c + 1) * 128], ident[:])
                                for c in range(4)], [b_stage, b_const], [b_pb[bk]])
                            P.op("vector", lambda e, h4=h4, bk=bk, tc=tc: e.tensor_copy(
                                out=cKT[:, h4 * 4:(h4 + 1) * 4, tc * 128:(tc + 1) * 128], in_=pbv(bk, 4, 128)), [b_pb[bk]], [b_kown])
                    cut(42, kind == "B")
                    Vh = [bigv(OFF_V + i * 2304, 18, 128) for i in range(2)]
                    Vhf = [bass.AP(bigf, (OFF_V + i * 2304) // 2, [[BIG_E // 2, 128], [64, 18], [1, 64]]) for i in range(2)]

                    def load_head(kvh, n):
                        i = n % 2
                        ksp, kloc = kvh // HPS, kvh % HPS
                        P.dma("sync", lambda e: e.dma_start(
                            out=bass.AP(kThf[i], 0, [[1024, 128], [256, 4], [1, 256]]),
                            in_=kx[(kind, "K", "all", ksp)].ap().rearrange("(r p) n -> p r n", p=128)[:, :, kloc * 256:(kloc + 1) * 256]),
                            [b_kvall[kind]], [b_kth[i]])
                        wr = [b_vh[i]] + ([b_vown] if n < 2 else [])
                        P.dma("sync", [lambda e, r=r, sp=sp: e.dma_start(
                            out=Vhf[i][:, 2 + 4 * r + sp * TPS:2 + 4 * r + (sp + 1) * TPS, :],
                            in_=kx[(kind, "V", "all", sp)].ap()[r * 128:(r + 1) * 128, :].rearrange(
                                "p (c hh w) -> p c hh w", c=TPS, hh=NK, w=64)[:, :, kvh, :])
                            for r in range(4) for sp in range(nsp)], [b_kvall[kind]], wr)
                        P.dma("gpsimd", lambda e: e.dma_start(
                            out=Vh[i][:, 0:2, :], in_=cv_d.rearrange("(c p) n -> p c n", p=128)[:, :, kvh * 128:(kvh + 1) * 128]),
                            [], [b_vh[i]])
                    nload = [0]
                    load_head(0, 0)
                    nload[0] = 1
                    load_head(1, 1)
                    nload[0] = 2
                ckpt(mtag + "_xch")
                def att_core(rhs_ap, rbufs, ob, db, nq, chunks, kvb):
                    nch = len(chunks)
                    LA = 2
                    eis = {}
                    for step in range(nch + LA):
                        if step < nch:
                            ic = step
                            kTc = chunks[ic][0]
                            sbk = ic % 4
                            P.op("tensor", lambda e, kTc=kTc, sbk=sbk: e.matmul(
                                pb[sbk][:, 0:nq], lhsT=kTc, rhs=rhs_ap, start=True, stop=True),
                                kvb + rbufs, [b_pb[sbk]])
                            ei = E()
                            eis[ic] = ei
                            P.op("scalar", lambda e, ei=ei, sbk=sbk: e.activation(
                                out=ebuf[ei][:, 0:nq], in_=pb[sbk][:, 0:nq], func=AF.Exp, scale=sm_scale), [b_pb[sbk]], [b_eb[ei]])
                        ic = step - LA
                        if ic >= 0:
                            Vc = chunks[ic][1]
                            ei = eis[ic]
                            P.op("tensor", [lambda e, Vc=Vc, ei=ei, ic=ic: e.matmul(
                                pb[ob][:, 0:nq], lhsT=Vc, rhs=ebuf[ei][:, 0:nq], start=(ic == 0), stop=(ic == nch - 1)),
                                lambda e, ei=ei, ic=ic: e.matmul(
                                pb[db][:, 0:nq], lhsT=onesbf[:], rhs=ebuf[ei][:, 0:nq], start=(ic == 0), stop=(ic == nch - 1))],
                                kvb + [b_eb[ei], b_const], [b_pb[ob], b_pb[db]])

                def att_A(h, q0, nq, chunks, kvb):
                    ob, db = 4, 6
                    att_core(qT[:, h, q0:q0 + nq], [b_q], ob, db, nq, chunks, kvb)
                    ri = S()
                    P.op("vector", [lambda e: e.reciprocal(out=scr[ri][:, 0:nq], in_=pb[db][:, 0:nq]),
                                    lambda e: e.tensor_tensor(out=hT[:, h, q0:q0 + nq], in0=pb[ob][:, 0:nq],
                                                              in1=scr[ri][:, 0:nq], op=ALU.mult)],
                         [b_pb[ob], b_pb[db]], [b_scr[ri], b_h])

                def att_B(h, q0, nq, chunks, kvb):
                    nch = len(chunks)
                    LA = 1
                    eis = {}
                    P.drain("tensor")
                    for step in range(nch + LA):
                        if step < nch:
                            ic = step
                            kTc = chunks[ic][0]
                            for m in range(2):
                                sbk = (ic % 2) * 2 + m
                                P.op("tensor", lambda e, kTc=kTc, sbk=sbk, m=m: e.matmul(
                                    pb[sbk][:, 0:nq], lhsT=kTc[m * 64:(m + 1) * 64, :], rhs=qT[m * 64:(m + 1) * 64, h, q0:q0 + nq],
                                    start=True, stop=True), kvb + [b_q], [b_pb[sbk]])
                                ei = E()
                                eis[(ic, m)] = ei
                                P.op("scalar", lambda e, ei=ei, sbk=sbk: e.activation(
                                    out=ebuf[ei][:, 0:nq], in_=pb[sbk][:, 0:nq], func=AF.Exp, scale=sm_scale), [b_pb[sbk]], [b_eb[ei]])
                            P.drain("tensor")
                        ic = step - LA
                        if ic >= 0:
                            Vc = chunks[ic][1]
                            for m in range(2):
                                ei = eis[(ic, m)]
                                P.op("tensor", [lambda e, Vc=Vc, ei=ei, ic=ic, m=m: e.matmul(
                                    pb[4 + m][:, 0:nq], lhsT=Vc, rhs=ebuf[ei][:, 0:nq], start=(ic == 0), stop=(ic == nch - 1)),
                                    lambda e, ei=ei, ic=ic, m=m: e.matmul(
                                    pb[6 + m][:, 0:nq], lhsT=onesbf[:], rhs=ebuf[ei][:, 0:nq], start=(ic == 0), stop=(ic == nch - 1))],
                                    kvb + [b_eb[ei], b_const], [b_pb[4 + m], b_pb[6 + m]])
                            P.drain("tensor")
                    r0, r1, oi, si = S(), S(), S(), S()
                    P.op("vector", [lambda e: e.reciprocal(out=scr[r0][:, 0:nq], in_=pb[6][:, 0:nq]),
                                    lambda e: e.reciprocal(out=scr[r1][:, 0:nq], in_=pb[7][:, 0:nq]),
                                    lambda e: e.tensor_tensor(out=scr[r0][:, 0:nq], in0=pb[4][:, 0:nq], in1=scr[r0][:, 0:nq], op=ALU.mult),
                                    lambda e: e.tensor_tensor(out=scr[r1][:, 0:nq], in0=pb[5][:, 0:nq], in1=scr[r1][:, 0:nq], op=ALU.mult),
                                    lambda e: e.scalar_tensor_tensor(out=scr[oi][:, 0:nq], in0=scr[r1][:, 0:nq], scalar=neglam_col,
                                                                     in1=scr[r0][:, 0:nq], op0=ALU.mult, op1=ALU.add)],
                         [b_pb[4], b_pb[5], b_pb[6], b_pb[7], b_small], [b_scr[r0], b_scr[r1], b_scr[oi]])
                    P.op("scalar", lambda e: e.activation(out=scr[si][:, 0:nq], in_=scr[oi][:, 0:nq], func=AF.Square),
                         [b_scr[oi]], [b_scr[si]])
                    P.op("tensor", lambda e: e.matmul(pb[0][:, 0:nq], lhsT=ones32[:], rhs=scr[si][:, 0:nq], start=True, stop=True),
                         [b_scr[si], b_const], [b_pb[0]])
                    P.op("scalar", lambda e: e.activation(out=scr[si][:, 0:nq], in_=pb[0][:, 0:nq], func=AF.Sqrt,
                                                          bias=eps_col, scale=1.0 / 128), [b_pb[0], b_small], [b_scr[si]])
                    P.op("vector", [lambda e: e.reciprocal(out=scr[si][:, 0:nq], in_=scr[si][:, 0:nq]),
                                    lambda e: e.tensor_tensor(out=scr[oi][:, 0:nq], in0=scr[oi][:, 0:nq], in1=scr[si][:, 0:nq], op=ALU.mult)],
                         [b_scr[si], b_scr[oi]], [b_scr[si], b_scr[oi]])
                    P.op("scalar", lambda e: e.activation(out=hT[:, h, q0:q0 + nq], in_=scr[oi][:, 0:nq], func=AF.Identity,
                                                          scale=subg_col), [b_scr[oi], b_small], [b_h])

                for h in range(16):
                    kvh = h // 4 if kind == "A" else h
                    if sample:
                        n = kvh
                        i = n % 2
                        chunks = [(cKT[:, kvh, 0:128], Vh[i][:, 0, :]), (cKT[:, kvh, 128:256], Vh[i][:, 1, :])]
                        chunks += [(kTh[i][:, kc * 128:(kc + 1) * 128], Vh[i][:, 2 + kc, :]) for kc in range(16)]
                        groups = [(0, T, chunks, [b_kown, b_kth[i], b_vh[i]])]
                    else:
                        groups = []
                        for sq in range(2):
                            chunks = [(kTown[:, kvh, (2 * sq + c) * 128:(2 * sq + c + 1) * 128],
                                       Vown[:, 2 * sq + c, kvh * 128:(kvh + 1) * 128]) for c in range(2)]
                            groups.append((sq * 256, 256, chunks, [b_kown, b_vown]))
                    if False:
                        qi_ = h % 2
                        P.op("vector", [lambda e, h=h, qi_=qi_: e.tensor_copy(out=qpad[qi_][0:64, 0, :], in_=qT[0:64, h, :]),
                                        lambda e, h=h, qi_=qi_: e.tensor_copy(out=qpad[qi_][64:128, 1, :], in_=qT[64:128, h, :])],
                             [b_q], [b_qp[qi_]])
                    for (q0, nq, chunks, kvb) in groups:
                        if kind == "A":
                            att_A(h, q0, nq, chunks, kvb)
                        else:
                            att_B(h, q0, nq, chunks, kvb)
                    cut(36, h == 0)
                    cut(37, h == 1)
                    cut(38, h == 3)
                    cut(39, h == 7)
                    if sample:
                        last_of_kvh = (kind == "B") or (h % 4 == 3)
                        nxt = nload[0]
                        if last_of_kvh and nxt < NK:
                            load_head(nxt, nxt)
                            nload[0] += 1
                ckpt(mtag + "_att")
                for obk in range(4):
                    def loader(sl, obk=obk):
                        return [lambda e: e.dma_start(out=sv(sl, 16, 512),
                                                      in_=wo[:, obk * 512:(obk + 1) * 512].rearrange("(k p) n -> p k n", p=128))]
                    sl, sbuf = ws.next(loader)
                    wv = sv(sl, 16, 512)
                    for c4 in range(4):
                        fc = obk * 4 + c4
                        yb = c4 % 2
                        P.op("tensor", [lambda e, kc=kc, wv=wv, c4=c4, yb=yb: e.matmul(
                            pb[yb][:], lhsT=wv[:, kc, c4 * 128:(c4 + 1) * 128], rhs=hT[:, kc, :],
                            start=(kc == 0), stop=(kc == KC - 1)) for kc in range(KC)], [sbuf, b_h], [b_pb[yb]])
                        P.op("vector", lambda e, fc=fc, yb=yb: e.scalar_tensor_tensor(
                            out=xT[:, fc, :], in0=pb[yb][:], scalar=mcol(modG1, s, 2, fc, l, ci), in1=xT[:, fc, :],
                            op0=ALU.mult, op1=ALU.add), [b_pb[yb], b_x[fc], b_mod], [b_x[fc]])
                layernorm(l, s)

            for (ci, sample, src, dst, ob) in [(0, False, xp_d, yp_d, b_out["yp"]), (1, True, xs_d, ys_d, b_out["ys"])]:
                tag = "s" if sample else "p"
                load_x(src)
                ckpt(tag + "load")
                for l in range(2):
                    ffn(l, 0, ci, 0)
                    ckpt(tag + "ffn0_%d" % l)
                    mixer(l, ci, sample, "A" if l == 0 else "B")
                    ckpt(tag + "mix_%d" % l)
                    ffn(l, 1, ci, 2)
                    ckpt(tag + "ffn1_%d" % l)
                store_x(dst, ob)
                ckpt(tag + "store")
            P.wait_all("sync", list(b_out.values()))

        dry = WStream(Prog(), b_slot)
        for b in ([b_h, b_q, b_kown, b_vown, b_stage, b_mod, b_rope, b_const, b_small, b_cond, b_scT, b_mpart,
                   b_modin, b_modout] + b_x + b_act + b_slot + b_kth + b_vh + b_scr + b_eb + b_pb
                  + list(b_kvin.values()) + list(b_kvall.values()) + list(b_out.values())):
            pass
        try:
            program(dry.P, dry)
        except _Stop:
            pass
        allb = ([b_h, b_q, b_kown, b_vown, b_stage, b_mod, b_rope, b_const, b_small, b_cond, b_scT, b_mpart,
                 b_modin, b_modout] + b_x + b_act + b_slot + b_kth + b_vh + b_scr + b_eb + b_pb
                + list(b_kvin.values()) + list(b_kvall.values()) + list(b_out.values()))
        for b in allb:
            b.w = None
            b.r = []
        P = Prog()
        ws = WStream(P, b_slot, dry.loaders)
        try:
            program(P, ws)
        except _Stop:
            P.wait_all("sync", allb)
        assert ws.i == len(dry.loaders), (ws.i, len(dry.loaders))
        P.emit(nc, st)
    return nc


def _rope_tables(qtr):
    n = np.arange(qtr * T, (qtr + 1) * T)
    row = (n // 64).astype(np.float32)
    col = (n % 64).astype(np.float32)
    out = []
    for dim in (128, 64):
        quarter = dim // 4
        inv = (np.float32(10000.0) ** (-np.arange(quarter, dtype=np.float32) / np.float32(quarter))).astype(np.float32)
        ang = np.concatenate([row[:, None] * inv, col[:, None] * inv], axis=-1).astype(np.float32)
        half = dim // 2
        cosT = np.cos(ang).astype(np.float32).T
        sinT = np.sin(ang).astype(np.float32).T
        reps = 128 // half
        out.append(np.tile(cosT, (reps, 1)))
        out.append(np.tile(sinT, (reps, 1)))
    return np.ascontiguousarray(np.concatenate(out, axis=1), dtype=np.float32)


def _rot_mats():
    out = []
    for half in (64, 32):
        R = np.zeros((128, 128), np.float32)
        for i in range(128):
            blk = (i // (2 * half)) * 2 * half
            r = i - blk
            if r < half:
                R[i, blk + r + half] = -1.0
            else:
                R[i, blk + r - half] = 1.0
        out.append(R.T)
    return np.ascontiguousarray(np.concatenate(out, axis=1), dtype=np.float32)


def kernel(x_prompt, x_sample, cache_a_k, cache_a_v, cache_b_k, cache_b_v, c, c_ctx,
           ada_w, ada_b, ln_g, ln_b, ffn_w_in, ffn_w_out,
           a_w_qkv, a_q_norm, a_k_norm, a_w_o,
           b_w_qkv, b_lambda, b_subln, b_w_o):
    f = lambda a: np.ascontiguousarray(np.asarray(a), dtype=np.float32)
    x_prompt, x_sample = f(x_prompt), f(x_sample)
    ada_w, ada_b = f(ada_w), f(ada_b)
    ln_g, ln_b = f(ln_g), f(ln_b)
    w_in, w_out = f(ffn_w_in), f(ffn_w_out)
    aqkv, awo, bqkv, bwo = f(a_w_qkv)[0], f(a_w_o)[0], f(b_w_qkv)[0], f(b_w_o)[0]
    c, c_ctx = f(c), f(c_ctx)
    lnT = np.stack([ln_g, ln_b], 0).reshape(2, 6, 16, 128).transpose(3, 0, 1, 2).reshape(128, 192)
    gains = np.stack([f(a_q_norm)[0], f(a_k_norm)[0], f(b_subln)[0]], 1)
    blam = np.tile(f(b_lambda).reshape(1, 256), (128, 1))
    rot = _rot_mats()
    in_maps = []
    for r in range(NCORES):
        b, qtr = r // 4, r % 4
        condT = np.stack([c_ctx.reshape(16, 128).T, c[b].reshape(16, 128).T], -1).reshape(128, 32)
        adaw_s = ada_w[:, :, qtr * 4608:(qtr + 1) * 4608]
        adab_s = ada_b[:, qtr * 4608:(qtr + 1) * 4608].reshape(2, 36, 128).transpose(2, 1, 0).reshape(128, 72)
        in_maps.append({
            "xp": x_prompt[2 * r:2 * r + 2].reshape(T, D),
            "xs": x_sample[b, qtr * T:(qtr + 1) * T],
            "cak": f(cache_a_k)[b, 0].reshape(256, 512), "cav": f(cache_a_v)[b, 0].reshape(256, 512),
            "cbk": f(cache_b_k)[b, 0].reshape(256, 2048), "cbv": f(cache_b_v)[b, 0].reshape(256, 2048),
            "condT": f(condT), "adaw": f(adaw_s), "adab": f(adab_s), "lnT": f(lnT),
            "w_in": w_in, "w_out": w_out, "a_qkv": aqkv, "a_wo": awo, "b_qkv": bqkv, "b_wo": bwo,
            "gains": f(gains), "blam": f(blam), "rope": _rope_tables(qtr), "rot": rot,
        })
    if KFAKE:
        for m in in_maps:
            for k in ("w_in", "w_out", "a_qkv", "a_wo", "b_qkv", "b_wo", "adaw"):
                del m[k]
    nc = build_nc()
    res = run_bass_kernel_spmd(nc, in_maps, core_ids=list(range(NCORES)))
    R = res.results
    y_prompt = np.concatenate([R[r]["yp"].reshape(2, 256, D) for r in range(NCORES)], 0)
    y_sample = np.stack([np.concatenate([R[b * 4 + q]["ys"] for q in range(4)], 0) for b in range(2)], 0)
    nak = np.concatenate([R[r]["nak"].reshape(2, 1, 256, 4, 128) for r in range(NCORES)], 0)
    nav = np.concatenate([R[r]["nav"].reshape(2, 1, 256, 4, 128) for r in range(NCORES)], 0)
    nbk = np.concatenate([R[r]["nbk"].reshape(2, 1, 256, 16, 2, 64) for r in range(NCORES)], 0)
    nbv = np.concatenate([R[r]["nbv"].reshape(2, 1, 256, 16, 128) for r in range(NCORES)], 0)
    return (y_prompt.astype(np.float32), y_sample.astype(np.float32), nak.astype(np.float32),
            nav.astype(np.float32), nbk.astype(np.float32), nbv.astype(np.float32))
```

```python
import math
import numpy as np
from contextlib import ExitStack
import concourse.bass as bass
import concourse.mybir as mybir
from concourse.bass_utils import run_bass_kernel_spmd

F32 = mybir.dt.float32
BF16 = mybir.dt.bfloat16
AF = mybir.ActivationFunctionType
ALU = mybir.AluOpType
AX = mybir.AxisListType

NCORES = 8
D = 2048
KC = 16
T = 512
DFF = 5504
NJ = 43
ALPHA = 4.0 ** 0.25
EPS_LN = 1e-6 / (ALPHA * ALPHA)
EPS = 1e-6
LAM_INIT = 0.8 - 0.6 * math.exp(-0.3 * 1)
NSLOT = 3
SLOT_E = 8192
BIG_E = 24576
ENGS = ("tensor", "vector", "scalar", "gpsimd", "sync")
RG = [[0, 1, 2, 3], [4, 5, 6, 7]]
import os
STOP_AFTER = os.environ.get("KSTOP", "")
KFAKE = bool(os.environ.get("KFAKE", ""))


class _Stop(Exception):
    pass


def ckpt(name):
    if STOP_AFTER == name:
        raise _Stop()


KCUT = int(os.environ.get("KCUT", "0"))
KH = int(os.environ.get("KH", "0"))
KQP = int(os.environ.get("KQP", "3"))


def cut(n, cond=True):
    if KCUT == n and cond:
        raise _Stop()


class Buf:
    def __init__(self, name):
        self.name = name
        self.w = None
        self.r = []


class Prog:
    def __init__(self):
        self.ops = {e: [] for e in ENGS}
        self.cnt = {}
        self.waited = {e: {} for e in ENGS}
        self.semkeys = []

    def _sem(self, key):
        if key not in self.cnt:
            self.cnt[key] = 0
            self.semkeys.append(key)
        return key

    def _deps(self, eng, reads, writes):
        evs = []
        for b in reads:
            if b.w is not None:
                evs.append(b.w)
        for b in writes:
            if b.w is not None:
                evs.append(b.w)
            evs.extend(b.r)
        own = "e_" + eng
        for (k, v) in evs:
            if k == own:
                continue
            if self.waited[eng].get(k, 0) < v:
                self.waited[eng][k] = v
                self.ops[eng].append(("wait", k, v))

    def op(self, eng, fns, reads=(), writes=()):
        if not isinstance(fns, (list, tuple)):
            fns = [fns]
        self._deps(eng, reads, writes)
        key = self._sem("e_" + eng)
        self.cnt[key] += 1
        ev = (key, self.cnt[key])
        for f in fns[:-1]:
            self.ops[eng].append(("op", f, None, 0))
        self.ops[eng].append(("op", fns[-1], key, 1))
        for b in writes:
            b.w = ev
            b.r = []
        for b in reads:
            b.r.append(ev)
        return ev

    def dma(self, eng, fns, reads, writes):
        if not isinstance(fns, (list, tuple)):
            fns = [fns]
        self._deps(eng, reads, writes)
        key = self._sem("d_" + writes[0].name)
        for f in fns:
            self.cnt[key] += 16
            self.ops[eng].append(("op", f, key, 16))
        ev = (key, self.cnt[key])
        for b in writes:
            b.w = ev
            b.r = []
        for b in reads:
            b.r.append(ev)
        return ev

    def wait_all(self, eng, bufs):
        self._deps(eng, bufs, ())

    def drain(self, eng):
        key = "e_" + eng
        v = self.cnt.get(key, 0)
        if v > 0:
            self.ops[eng].append(("wait", key, v))

    def emit(self, nc, stack):
        sems = {}
        for k in self.semkeys:
            sems[k] = stack.enter_context(nc.semaphore(k))
        block = stack.enter_context(nc.Block())

        def replay(name):
            def run(e):
                for item in self.ops[name]:
                    if item[0] == "wait":
                        e.wait_ge(sems[item[1]], item[2])
                    else:
                        ins = item[1](e)
                        if item[2] is not None:
                            ins.then_inc(sems[item[2]], item[3])
            return run

        for name in ENGS:
            if self.ops[name]:
                getattr(block, name)(replay(name))


class WStream:
    def __init__(self, P, slot_bufs, loaders=None):
        self.P = P
        self.bufs = slot_bufs
        self.dry = loaders is None
        self.loaders = [] if self.dry else loaders
        self.i = 0
        self.issued = 0

    def _issue(self, upto):
        while self.issued < min(upto, len(self.loaders)):
            n = self.issued
            s = n % NSLOT
            self.P.dma("gpsimd", self.loaders[n](s), [], [self.bufs[s]])
            self.issued += 1

    def next(self, loader):
        if self.dry:
            self.loaders.append(loader)
            return 0, self.bufs[0]
        i = self.i
        self._issue(i + NSLOT)
        self.i += 1
        return i % NSLOT, self.bufs[i % NSLOT]


def build_nc():
    nc = bass.Bass("TRN2", target_bir_lowering=False)

    def din(name, shape):
        if KFAKE and name in ("w_in", "w_out", "a_qkv", "a_wo", "b_qkv", "b_wo", "adaw"):
            return nc.dram_tensor(name, list(shape), F32).ap()
        return nc.dram_tensor(name, list(shape), F32, kind="ExternalInput").ap()

    def dout(name, shape):
        return nc.dram_tensor(name, list(shape), F32, kind="ExternalOutput").ap()

    xp_d = din("xp", [T, D])
    xs_d = din("xs", [T, D])
    cak_d = din("cak", [256, 512])
    cav_d = din("cav", [256, 512])
    cbk_d = din("cbk", [256, 2048])
    cbv_d = din("cbv", [256, 2048])
    cond_d = din("condT", [128, 32])
    adaw_d = din("adaw", [2, D, 4608])
    adab_d = din("adab", [128, 72])
    lnT_d = din("lnT", [128, 192])
    if KFAKE:
        class _W4:
            def __init__(self):
                self.t = {(l, f): nc.dram_tensor("w_in%d%d" % (l, f), [D, 2 * DFF], F32).ap() for l in range(2) for f in range(2)}

            def __getitem__(self, k):
                return self.t[k]
        win_d = _W4()
    else:
        win_d = din("w_in", [2, 2, D, 2 * DFF])
    wout_d = din("w_out", [2, 2, DFF, D])
    aqkv_d = din("a_qkv", [D, 3072])
    awo_d = din("a_wo", [D, D])
    bqkv_d = din("b_qkv", [D, 6144])
    bwo_d = din("b_wo", [D, D])
    gains_d = din("gains", [128, 3])
    blam_d = din("blam", [128, 256])
    rope_d = din("rope", [128, 4 * T])
    rot_d = din("rot", [128, 256])

    yp_d = dout("yp", [T, D])
    ys_d = dout("ys", [T, D])
    nak_d = dout("nak", [T, 512])
    nav_d = dout("nav", [T, 512])
    nbk_d = dout("nbk", [T, 2048])
    nbv_d = dout("nbv", [T, 2048])

    mod_in = nc.dram_tensor("mod_in", [128, 144], F32)
    mod_out = nc.dram_tensor("mod_out", [512, 144], F32)
    NSP = {"A": 1, "B": 4}
    kx = {}
    for kd, kw in (("A", 1024), ("B", 4096)):
        for part in ("K", "V"):
            for sp in range(NSP[kd]):
                kx[(kd, part, "in", sp)] = nc.dram_tensor("kx%s%s_in%d" % (kd, part, sp), [128, kw // NSP[kd]], F32)
                kx[(kd, part, "all", sp)] = nc.dram_tensor("kx%s%s_all%d" % (kd, part, sp), [512, kw // NSP[kd]], F32)

    with ExitStack() as st:
        def sb(name, shape, dt):
            return st.enter_context(nc.sbuf_tensor(name, list(shape), dt))

        def ps(name, shape, dt):
            return st.enter_context(nc.psum_tensor(name, list(shape), dt))

        xT = sb("xT", [128, KC, T], F32)
        hT = sb("hT", [128, KC, T], BF16)
        big = sb("big", [128, BIG_E], BF16)
        bigf = big.bitcast(F32)
        slots = [sb("slot%d" % i, [128, SLOT_E], BF16) for i in range(NSLOT)]
        kTh = [sb("kTh%d" % i, [128, 2048], BF16) for i in range(2)]
        kThf = [t.bitcast(F32) for t in kTh]
        stage = sb("stage", [128, D], F32)
        NSCR = 6
        scr = [sb("scr%d" % i, [128, T], F32) for i in range(NSCR)]
        ebuf = [sb("eb%d" % i, [128, T], BF16) for i in range(4)]
        modR = sb("modR", [128, 576], F32)
        modS = sb("modS", [128, 576], F32)
        modG1 = sb("modG1", [128, 576], F32)
        modG5 = sb("modG5", [128, 576], F32)
        ropet = sb("ropet", [128, 4 * T], F32)
        ident = sb("ident", [128, 128], F32)
        ones32 = sb("ones32", [128, 128], F32)
        onesbf = sb("onesbf", [128, 128], BF16)
        rott = sb("rott", [128, 256], F32)
        condt = sb("condt", [128, 32], F32)
        scT = sb("scT", [128, 32], BF16)
        lnT = sb("lnT_t", [128, 192], F32)
        adab = sb("adab_t", [128, 72], F32)
        mpart = sb("mpart", [128, 144], F32)
        gains = sb("gains_t", [128, 3], F32)
        lamt = sb("lamt", [128, 256], F32)
        small = sb("small", [128, 16], F32)
        pb = [ps("pb%d" % i, [128, T], F32) for i in range(8)]

        b_x = [Buf("x%d" % k) for k in range(KC)]
        b_h = Buf("hT")
        b_act = [Buf("act%d" % j) for j in range(NJ)]
        b_q, b_kown, b_vown = Buf("q"), Buf("kown"), Buf("vown")
        b_slot = [Buf("slot%d" % i) for i in range(NSLOT)]
        b_kth = [Buf("kth%d" % i) for i in range(2)]
        b_vh = [Buf("vh%d" % i) for i in range(2)]
        b_stage = Buf("stage")
        b_scr = [Buf("scr%d" % i) for i in range(NSCR)]
        b_eb = [Buf("eb%d" % i) for i in range(4)]
        b_qp = [Buf("qp%d" % i) for i in range(2)]
        b_mod, b_rope, b_const, b_small = Buf("mod"), Buf("rope"), Buf("const"), Buf("small")
        b_cond, b_scT, b_mpart = Buf("cond"), Buf("scT"), Buf("mpart")
        b_pb = [Buf("pb%d" % i) for i in range(8)]
        b_modin, b_modout = Buf("modin"), Buf("modout")
        b_kvin, b_kvall = {"A": Buf("kvAin"), "B": Buf("kvBin")}, {"A": Buf("kvAall"), "B": Buf("kvBall")}
        b_vin, b_vall = {"A": Buf("vAin"), "B": Buf("vBin")}, {"A": Buf("vAall"), "B": Buf("vBall")}
        b_out = {n: Buf("o_" + n) for n in ["yp", "ys", "nak", "nav", "nbk", "nbv"]}

        def bigv(off, a, b):
            return bass.AP(big, off, [[BIG_E, 128], [b, a], [1, b]])

        def bigfv(off_e, n):
            return bass.AP(bigf, off_e // 2, [[BIG_E // 2, 128], [1, n // 2]])

        def sv(s, a, b):
            return bass.AP(slots[s], 0, [[SLOT_E, 128], [b, a], [1, b]])

        def pbv(i, a, b):
            return bass.AP(pb[i], 0, [[T, 128], [b, a], [1, b]])

        def scrv(i, a, b):
            return bass.AP(scr[i], 0, [[T, 128], [b, a], [1, b]])

        actT = bigv(0, NJ, T)
        qT = bigv(0, 16, T)
        kTown = bigv(8192, 16, T)
        OFF_K, OFF_V = 8192, 16384

        def mcol(tile, s, t, k, l, c):
            g = (s * 3 + t) * 16 + k
            col = (g * 2 + l) * 2 + c
            return tile[:, col:col + 1]

        def lncol(gb, l, s, k):
            col = (gb * 6 + l * 3 + s) * 16 + k
            return lnT[:, col:col + 1]

        eps_ln_col = small[:, 0:1]
        eps_col = small[:, 1:2]
        neglam_col = small[:, 2:3]
        subg_col = small[:, 3:4]

        def program(P, ws):
            sc_i = [0]

            def S():
                i = sc_i[0] % NSCR
                sc_i[0] += 1
                return i

            eb_i = [0]

            def E():
                i = eb_i[0] % 4
                eb_i[0] += 1
                return i

            P.op("vector", [lambda e: e.memset(ones32[:], 1.0),
                            lambda e: e.memset(onesbf[:], 1.0),
                            lambda e: e.memset(small[:, 0:1], EPS_LN),
                            lambda e: e.memset(small[:, 1:2], EPS)], [], [b_const, b_small])
            P.op("gpsimd", [lambda e: e.memset(ident[:], 0.0),
                            lambda e: e.affine_select(out=ident[:], in_=ident[:], pattern=[[-1, 128]],
                                                      compare_op=ALU.not_equal, fill=1.0, base=0,
                                                      channel_multiplier=1)], [], [b_const])
            P.dma("sync", [lambda e: e.dma_start(out=condt[:], in_=cond_d),
                           lambda e: e.dma_start(out=lnT[:], in_=lnT_d),
                           lambda e: e.dma_start(out=adab[:], in_=adab_d),
                           lambda e: e.dma_start(out=gains[:], in_=gains_d),
                           lambda e: e.dma_start(out=lamt[:], in_=blam_d),
                           lambda e: e.dma_start(out=rott[:], in_=rot_d),
                           lambda e: e.dma_start(out=ropet[:], in_=rope_d)], [], [b_cond])
            P.op("vector", [lambda e: e.tensor_tensor(out=scr[0][:, 0:64], in0=lamt[:, 0:64], in1=lamt[:, 64:128], op=ALU.mult),
                            lambda e: e.tensor_tensor(out=scr[0][:, 64:128], in0=lamt[:, 128:192], in1=lamt[:, 192:256], op=ALU.mult)],
                 [b_cond], [b_scr[0]])
            P.drain("vector")
            P.op("vector", lambda e: e.reduce_sum(out=small[:, 4:5], in_=scr[0][:, 0:64], axis=AX.X), [b_scr[0]], [b_small])
            P.op("vector", lambda e: e.reduce_sum(out=small[:, 5:6], in_=scr[0][:, 64:128], axis=AX.X), [b_scr[0]], [b_small])
            P.drain("vector")
            P.op("scalar", lambda e: e.activation(out=small[:, 6:8], in_=small[:, 4:6], func=AF.Exp), [b_small], [b_small])
            P.op("vector", lambda e: e.tensor_tensor(out=small[:, 8:9], in0=small[:, 7:8], in1=small[:, 6:7], op=ALU.subtract),
                 [b_small], [b_small])
            P.drain("vector")
            P.op("vector", lambda e: e.tensor_scalar(out=small[:, 2:3], in0=small[:, 8:9], scalar1=-LAM_INIT, scalar2=None, op0=ALU.add),
                 [b_small], [b_small])
            P.op("vector", lambda e: e.tensor_scalar(out=small[:, 3:4], in0=gains[:, 2:3], scalar1=1.0 - LAM_INIT, scalar2=None, op0=ALU.mult),
                 [b_small, b_cond], [b_small])
            P.drain("vector")

            ckpt("const")
            P.op("scalar", lambda e: e.activation(out=scT[:], in_=condt[:], func=AF.Silu), [b_cond], [b_scT])
            scT3 = bass.AP(scT, 0, [[32, 128], [2, 16], [1, 2]])
            for l in range(2):
                for wt in range(9):
                    def loader(s, l=l, wt=wt):
                        return [lambda e: e.dma_start(out=sv(s, 16, 512),
                                                      in_=adaw_d[l][:, wt * 512:(wt + 1) * 512].rearrange("(k p) n -> p k n", p=128))]
                    s, sbuf = ws.next(loader)
                    wv = sv(s, 16, 512)
                    for c4 in range(4):
                        cc = wt * 4 + c4
                        bk = cc % 2
                        P.op("tensor", [lambda e, kc=kc, wv=wv, c4=c4, bk=bk: e.matmul(
                            pb[bk][:, 0:2], lhsT=wv[:, kc, c4 * 128:(c4 + 1) * 128], rhs=scT3[:, kc, :],
                            start=(kc == 0), stop=(kc == KC - 1)) for kc in range(KC)], [sbuf, b_scT], [b_pb[bk]])
                        idx = (cc * 2 + l) * 2
                        P.op("vector", lambda e, bk=bk, idx=idx, cc=cc, l=l: e.tensor_scalar(
                            out=mpart[:, idx:idx + 2], in0=pb[bk][:, 0:2], scalar1=adab[:, cc * 2 + l:cc * 2 + l + 1],
                            scalar2=None, op0=ALU.add), [b_pb[bk], b_cond], [b_mpart])
            P.dma("sync", lambda e: e.dma_start(out=mod_in.ap(), in_=mpart[:]), [b_mpart], [b_modin])
            P.op("gpsimd", lambda e: e.collective_compute("AllGather", ALU.bypass, replica_groups=RG,
                                                          ins=[mod_in.ap()], outs=[mod_out.ap()]), [b_modin], [b_modout])
            P.dma("sync", lambda e: e.dma_start(out=bass.AP(modR, 0, [[576, 128], [144, 4], [1, 144]]),
                                                in_=mod_out.ap().rearrange("(r p) n -> p r n", p=128)), [b_modout], [b_mod])
            P.op("vector", [lambda e: e.tensor_scalar(out=modS[:], in0=modR[:], scalar1=1.0, scalar2=None, op0=ALU.add),
                            lambda e: e.tensor_scalar(out=modG1[:], in0=modR[:], scalar1=1.0 / ALPHA, scalar2=None, op0=ALU.mult),
                            lambda e: e.tensor_scalar(out=modG5[:], in0=modR[:], scalar1=0.5 / ALPHA, scalar2=None, op0=ALU.mult)],
                 [b_mod], [b_mod])

            ckpt("mod")
            def load_x(src):
                for t in range(4):
                    P.dma("sync", lambda e, t=t: e.dma_start(out=stage[:], in_=src[t * 128:(t + 1) * 128, :]), [], [b_stage])
                    for k4 in range(4):
                        bk = (t * 4 + k4) % 2
                        P.op("tensor", [lambda e, c=c, k4=k4, bk=bk: e.transpose(
                            pb[bk][:, c * 128:(c + 1) * 128], stage[:, (k4 * 4 + c) * 128:(k4 * 4 + c + 1) * 128], ident[:])
                            for c in range(4)], [b_stage, b_const], [b_pb[bk]])
                        P.op("vector", lambda e, t=t, k4=k4, bk=bk: e.tensor_copy(
                            out=xT[:, k4 * 4:(k4 + 1) * 4, t * 128:(t + 1) * 128], in_=pbv(bk, 4, 128)),
                            [b_pb[bk]], b_x[k4 * 4:(k4 + 1) * 4])

            def store_x(dst, ob):
                for t in range(4):
                    for k4 in range(4):
                        bk = (t * 4 + k4) % 2
                        P.op("tensor", [lambda e, c=c, k4=k4, bk=bk, t=t: e.transpose(
                            pb[bk][:, c * 128:(c + 1) * 128], xT[:, k4 * 4 + c, t * 128:(t + 1) * 128], ident[:])
                            for c in range(4)], b_x[k4 * 4:(k4 + 1) * 4] + [b_const], [b_pb[bk]])
                        P.op("vector", lambda e, k4=k4, bk=bk: e.tensor_copy(
                            out=stage[:, k4 * 512:(k4 + 1) * 512], in_=pb[bk][:]), [b_pb[bk]], [b_stage])
                    P.dma("sync", lambda e, t=t: e.dma_start(out=dst[t * 128:(t + 1) * 128, :], in_=stage[:]), [b_stage], [ob])

            def modulate(l, ci, s):
                for k in range(KC):
                    P.op("scalar", lambda e, k=k: e.activation(
                        out=hT[:, k, :], in_=xT[:, k, :], func=AF.Identity,
                        scale=mcol(modS, s, 1, k, l, ci), bias=mcol(modR, s, 0, k, l, ci)),
                        [b_x[k], b_mod], [b_h])

            def layernorm(l, s):
                for fc in range(KC):
                    zi = S()
                    P.op("scalar", lambda e, fc=fc, zi=zi: e.activation(out=scr[zi][:], in_=xT[:, fc, :], func=AF.Square),
                         [b_x[fc]], [b_scr[zi]])
                    P.op("tensor", lambda e, fc=fc: e.matmul(pb[6][:], lhsT=ones32[:], rhs=xT[:, fc, :],
                                                             start=(fc == 0), stop=(fc == KC - 1)), [b_x[fc], b_const], [b_pb[6]])
                    P.op("tensor", lambda e, fc=fc, zi=zi: e.matmul(pb[7][:], lhsT=ones32[:], rhs=scr[zi][:],
                                                                    start=(fc == 0), stop=(fc == KC - 1)), [b_scr[zi], b_const], [b_pb[7]])
                mi, vi, ni = S(), S(), S()
                P.op("vector", lambda e: e.tensor_scalar(out=scr[mi][:], in0=pb[6][:], scalar1=1.0 / D, scalar2=None, op0=ALU.mult),
                     [b_pb[6]], [b_scr[mi]])
                P.op("vector", lambda e: e.tensor_tensor(out=scr[ni][:], in0=scr[mi][:], in1=scr[mi][:], op=ALU.mult),
                     [b_scr[mi]], [b_scr[ni]])
                P.op("vector", lambda e: e.scalar_tensor_tensor(out=scr[vi][:], in0=pb[7][:], scalar=1.0 / D, in1=scr[ni][:],
                                                                op0=ALU.mult, op1=ALU.subtract), [b_pb[7], b_scr[ni]], [b_scr[vi]])
                P.op("scalar", lambda e: e.activation(out=scr[vi][:], in_=scr[vi][:], func=AF.Sqrt, bias=eps_ln_col, scale=1.0),
                     [b_scr[vi], b_small], [b_scr[vi]])
                P.op("vector", lambda e: e.reciprocal(out=scr[vi][:], in_=scr[vi][:]), [b_scr[vi]], [b_scr[vi]])
                P.op("vector", lambda e: e.scalar_tensor_tensor(out=scr[ni][:], in0=scr[mi][:], scalar=-1.0, in1=scr[vi][:],
                                                                op0=ALU.mult, op1=ALU.mult), [b_scr[mi], b_scr[vi]], [b_scr[ni]])
                for fc in range(KC):
                    P.op("vector", [lambda e, fc=fc: e.tensor_tensor(out=xT[:, fc, :], in0=xT[:, fc, :], in1=scr[vi][:], op=ALU.mult),
                                    lambda e, fc=fc: e.tensor_tensor(out=xT[:, fc, :], in0=xT[:, fc, :], in1=scr[ni][:], op=ALU.add)],
                         [b_x[fc], b_scr[vi], b_scr[ni]], [b_x[fc]])
                    P.op("scalar", lambda e, fc=fc: e.activation(out=xT[:, fc, :], in_=xT[:, fc, :], func=AF.Identity,
                                                                 scale=lncol(0, l, s, fc), bias=lncol(1, l, s, fc)),
                         [b_x[fc], b_cond], [b_x[fc]])

            def ffn(l, f, ci, s):
                modulate(l, ci, s)
                w_in = win_d[l, f]
                w_out = wout_d[l, f]
                for j2 in range(22):
                    nj = 2 if j2 < 21 else 1

                    def loader(sl, j2=j2, nj=nj):
                        c0 = j2 * 256
                        wv = sv(sl, 16, 512)
                        return [lambda e: e.dma_start(out=wv[:, :, 0:nj * 128],
                                                      in_=w_in[:, c0:c0 + nj * 128].rearrange("(k p) n -> p k n", p=128)),
                                lambda e: e.dma_start(out=wv[:, :, 256:256 + nj * 128],
                                                      in_=w_in[:, DFF + c0:DFF + c0 + nj * 128].rearrange("(k p) n -> p k n", p=128))]
                    sl, sbuf = ws.next(loader)
                    wv = sv(sl, 16, 512)
                    for jj in range(nj):
                        j = j2 * 2 + jj
                        ga, ua = (j % 2) * 2, (j % 2) * 2 + 1
                        P.op("tensor", [lambda e, kc=kc, wv=wv, jj=jj, ga=ga: e.matmul(
                            pb[ga][:], lhsT=wv[:, kc, jj * 128:(jj + 1) * 128], rhs=hT[:, kc, :],
                            start=(kc == 0), stop=(kc == KC - 1)) for kc in range(KC)], [sbuf, b_h], [b_pb[ga]])
                        P.op("tensor", [lambda e, kc=kc, wv=wv, jj=jj, ua=ua: e.matmul(
                            pb[ua][:], lhsT=wv[:, kc, 256 + jj * 128:256 + (jj + 1) * 128], rhs=hT[:, kc, :],
                            start=(kc == 0), stop=(kc == KC - 1)) for kc in range(KC)], [sbuf, b_h], [b_pb[ua]])
                        si = S()
                        P.op("scalar", lambda e, si=si, ga=ga: e.activation(out=scr[si][:], in_=pb[ga][:], func=AF.Silu),
                             [b_pb[ga]], [b_scr[si]])
                        P.op("vector", lambda e, si=si, ua=ua, j=j: e.tensor_tensor(
                            out=actT[:, j, :], in0=pb[ua][:], in1=scr[si][:], op=ALU.mult),
                            [b_pb[ua], b_scr[si]], [b_act[j]])
                for cb in range(8):
                    for half in range(2):
                        j0, njh = (0, 22) if half == 0 else (22, 21)

                        def loader(sl, cb=cb, j0=j0, njh=njh):
                            return [lambda e: e.dma_start(
                                out=sv(sl, 22, 256)[:, 0:njh, :],
                                in_=w_out[j0 * 128:(j0 + njh) * 128, cb * 256:(cb + 1) * 256].rearrange("(j p) n -> p j n", p=128))]
                        sl, sbuf = ws.next(loader)
                        wv = sv(sl, 22, 256)
                        for f2 in range(2):
                            yb = 4 + f2
                            P.op("tensor", [lambda e, jj=jj, wv=wv, yb=yb, f2=f2, j0=j0, half=half, njh=njh: e.matmul(
                                pb[yb][:], lhsT=wv[:, jj, f2 * 128:(f2 + 1) * 128], rhs=actT[:, j0 + jj, :],
                                start=(half == 0 and jj == 0), stop=(half == 1 and jj == njh - 1))
                                for jj in range(njh)], [sbuf] + b_act[j0:j0 + njh], [b_pb[yb]])
                    for f2 in range(2):
                        fc = cb * 2 + f2
                        yb = 4 + f2
                        P.op("vector", lambda e, fc=fc, yb=yb: e.scalar_tensor_tensor(
                            out=xT[:, fc, :], in0=pb[yb][:], scalar=mcol(modG5, s, 2, fc, l, ci), in1=xT[:, fc, :],
                            op0=ALU.mult, op1=ALU.add), [b_pb[yb], b_x[fc], b_mod], [b_x[fc]])
                layernorm(l, s)

            def mixer(l, ci, sample, kind):
                s = 1
                modulate(l, ci, s)
                if kind == "A":
                    NK, NVB, wqkv, vcol0, wo = 4, 1, aqkv_d, 2560, awo_d
                    ck_d, cv_d, nk_d, nv_d, ob_k, ob_v = cak_d, cav_d, nak_d, nav_d, b_out["nak"], b_out["nav"]
                    sm_scale = 128.0 ** -0.5
                    rT, cosT, sinT = rott[:, 0:128], ropet[:, 0:T], ropet[:, T:2 * T]
                else:
                    NK, NVB, wqkv, vcol0, wo = 16, 4, bqkv_d, 4096, bwo_d
                    ck_d, cv_d, nk_d, nv_d, ob_k, ob_v = cbk_d, cbv_d, nbk_d, nbv_d, b_out["nbk"], b_out["nbv"]
                    sm_scale = 64.0 ** -0.5
                    rT, cosT, sinT = rott[:, 128:256], ropet[:, 2 * T:3 * T], ropet[:, 3 * T:4 * T]
                VW = NK * 128
                Vown = bigv(OFF_V, 4, VW)
                NQ = 16
                for t4 in range((NQ + NK) // 4):
                    def loader(sl, t4=t4):
                        return [lambda e: e.dma_start(out=sv(sl, 16, 512),
                                                      in_=wqkv[:, t4 * 512:(t4 + 1) * 512].rearrange("(k p) n -> p k n", p=128))]
                    sl, sbuf = ws.next(loader)
                    wv = sv(sl, 16, 512)
                    for c4 in range(4):
                        c = t4 * 4 + c4
                        isq = c < NQ
                        h = c if isq else c - NQ
                        rb = c % 2
                        dst = qT[:, h, :] if isq else kTown[:, h, :]
                        dbuf = b_q if isq else b_kown
                        P.op("tensor", [lambda e, kc=kc, wv=wv, c4=c4, rb=rb: e.matmul(
                            pb[rb][:], lhsT=wv[:, kc, c4 * 128:(c4 + 1) * 128], rhs=hT[:, kc, :],
                            start=(kc == 0), stop=(kc == KC - 1)) for kc in range(KC)], [sbuf, b_h], [b_pb[rb]])
                        cut(1)
                        if kind == "A":
                            qi, si, ri = S(), S(), S()
                            P.op("scalar", [lambda e, si=si, rb=rb: e.activation(out=scr[si][:], in_=pb[rb][:], func=AF.Square),
                                            lambda e, qi=qi, rb=rb: e.activation(out=scr[qi][:], in_=pb[rb][:], func=AF.Copy)],
                                 [b_pb[rb]], [b_scr[si], b_scr[qi]])
                            cut(2)
                            P.op("tensor", lambda e, si=si, rb=rb: e.matmul(pb[2 + rb][:], lhsT=ones32[:], rhs=scr[si][:],
                                                                            start=True, stop=True), [b_scr[si], b_const], [b_pb[2 + rb]])
                            cut(3)
                            P.op("scalar", lambda e, ri=ri, rb=rb: e.activation(out=scr[ri][:], in_=pb[2 + rb][:], func=AF.Sqrt,
                                                                                bias=eps_col, scale=1.0 / 128), [b_pb[2 + rb], b_small], [b_scr[ri]])
                            cut(4)
                            P.op("vector", [lambda e, ri=ri: e.reciprocal(out=scr[ri][:], in_=scr[ri][:]),
                                            lambda e, ri=ri, qi=qi: e.tensor_tensor(out=scr[qi][:], in0=scr[qi][:], in1=scr[ri][:], op=ALU.mult)],
                                 [b_scr[ri], b_scr[qi]], [b_scr[ri], b_scr[qi]])
                            cut(5)
                            gcol = gains[:, 0:1] if isq else gains[:, 1:2]
                            src, srcb = scr[qi][:], b_scr[qi]
                        else:
                            gcol = 1.0
                            src, srcb = pb[rb][:], b_pb[rb]
                        if sample:
                            xi, t1, t2 = S(), S(), S()
                            P.op("scalar", lambda e, xi=xi, src=src, gcol=gcol: e.activation(
                                out=scr[xi][:], in_=src, func=AF.Identity, scale=gcol), [srcb, b_cond], [b_scr[xi]])
                            P.op("tensor", lambda e, xi=xi, rb=rb: e.matmul(pb[4 + rb][:], lhsT=rT, rhs=scr[xi][:], start=True, stop=True),
                                 [b_scr[xi], b_cond], [b_pb[4 + rb]])
                            P.op("vector", lambda e, xi=xi, t1=t1: e.tensor_tensor(out=scr[t1][:], in0=scr[xi][:], in1=cosT, op=ALU.mult),
                                 [b_scr[xi], b_cond], [b_scr[t1]])
                            P.op("vector", lambda e, t2=t2, rb=rb: e.tensor_tensor(out=scr[t2][:], in0=pb[4 + rb][:], in1=sinT, op=ALU.mult),
                                 [b_pb[4 + rb], b_cond], [b_scr[t2]])
                            P.op("vector", lambda e, t1=t1, t2=t2, dst=dst: e.tensor_tensor(out=dst, in0=scr[t1][:], in1=scr[t2][:], op=ALU.add),
                                 [b_scr[t1], b_scr[t2]], [dbuf])
                        else:
                            P.op("scalar", lambda e, src=src, gcol=gcol, dst=dst: e.activation(
                                out=dst, in_=src, func=AF.Identity, scale=gcol), [srcb, b_cond], [dbuf])
                            cut(6)
                            if not isq:
                                ki, ko = S(), S()
                                P.op("scalar", lambda e, ki=ki, src=src, gcol=gcol: e.activation(
                                    out=scr[ki][:], in_=src, func=AF.Identity, scale=gcol), [srcb, b_cond], [b_scr[ki]])
                                cut(7)
                                P.op("tensor", [lambda e, t=t, ki=ki, rb=rb: e.transpose(
                                    pb[6 + rb][:, t * 128:(t + 1) * 128], scr[ki][:, t * 128:(t + 1) * 128], ident[:])
                                    for t in range(4)], [b_scr[ki], b_const], [b_pb[6 + rb]])
                                cut(8)
                                P.op("vector", lambda e, ko=ko, rb=rb: e.tensor_copy(out=scr[ko][:], in_=pb[6 + rb][:]),
                                     [b_pb[6 + rb]], [b_scr[ko]])
                                cut(9)
                                P.dma("sync", lambda e, ko=ko, h=h: e.dma_start(
                                    out=nk_d[:, h * 128:(h + 1) * 128].rearrange("(t p) d -> p t d", p=128), in_=scrv(ko, 4, 128)),
                                    [b_scr[ko]], [ob_k])
                                cut(10)
                        cut(11, c == 17)
                        cut(12, c == 18)
                        cut(13, c == 19)
                mtag = ("s" if sample else "p") + kind
                ckpt(mtag + "_qk")
                if sample:
                    nsp = NSP[kind]
                    HPS, TPS = NK // nsp, 4 // nsp
                    P.dma("sync", [lambda e, sp=sp: e.dma_start(out=kx[(kind, "K", "in", sp)].ap(),
                                                                 in_=bigfv(OFF_K + sp * HPS * T, HPS * T)) for sp in range(nsp)],
                          [b_kown], [b_kvin[kind]])
                    for sp in range(nsp):
                        P.op("gpsimd", lambda e, sp=sp: e.collective_compute(
                            "AllGather", ALU.bypass, replica_groups=RG,
                            ins=[kx[(kind, "K", "in", sp)].ap()], outs=[kx[(kind, "K", "all", sp)].ap()]),
                            [b_kvin[kind]], [b_kvall[kind]])
                for vb in range(NVB):
                    def loader(sl, vb=vb):
                        return [lambda e: e.dma_start(out=sv(sl, 16, 512),
                                                      in_=wqkv[:, vcol0 + vb * 512:vcol0 + (vb + 1) * 512].rearrange("(k p) n -> p k n", p=128))]
                    sl, sbuf = ws.next(loader)
                    wv = sv(sl, 16, 512)
                    for t in range(4):
                        vbk = 6 + t % 2
                        P.op("tensor", [lambda e, kc=kc, wv=wv, t=t, vbk=vbk: e.matmul(
                            pb[vbk][:], lhsT=hT[:, kc, t * 128:(t + 1) * 128], rhs=wv[:, kc, :],
                            start=(kc == 0), stop=(kc == KC - 1)) for kc in range(KC)], [sbuf, b_h], [b_pb[vbk]])
                        vi = S()
                        P.op("vector", lambda e, vi=vi, vbk=vbk: e.tensor_copy(out=scr[vi][:], in_=pb[vbk][:]),
                             [b_pb[vbk]], [b_scr[vi]])
                        P.op("scalar", lambda e, t=t, vb=vb, vi=vi: e.activation(
                            out=Vown[:, t, vb * 512:(vb + 1) * 512], in_=scr[vi][:], func=AF.Copy), [b_scr[vi]], [b_vown])
                        if not sample:
                            P.dma("sync", lambda e, vi=vi, t=t, vb=vb: e.dma_start(
                                out=nv_d[t * 128:(t + 1) * 128, vb * 512:(vb + 1) * 512], in_=scr[vi][:]), [b_scr[vi]], [ob_v])
                ckpt(mtag + "_v")
                KW = NK * 256
                if sample:
                    P.dma("sync", [lambda e, sp=sp: e.dma_start(out=kx[(kind, "V", "in", sp)].ap(),
                                                                 in_=bigfv(OFF_V + sp * TPS * VW, TPS * VW)) for sp in range(nsp)],
                          [b_vown], [b_vin[kind]])
                    for part in ("V",):
                        for sp in range(nsp):
                            P.op("gpsimd", lambda e, part=part, sp=sp: e.collective_compute(
                                "AllGather", ALU.bypass, replica_groups=RG,
                                ins=[kx[(kind, part, "in", sp)].ap()], outs=[kx[(kind, part, "all", sp)].ap()]),
                                [b_vin[kind]], [b_vall[kind]])
                    cut(41, kind == "B")
                    cKT = bigv(OFF_K, NK, 256)
                    for tc in range(2):
                        P.dma("sync", lambda e, tc=tc: e.dma_start(out=stage[:, 0:VW], in_=ck_d[tc * 128:(tc + 1) * 128, :]), [], [b_stage])
                        for h4 in range(NK // 4):
                            bk = h4 % 2
                            P.op("tensor", [lambda e, c=c, h4=h4, bk=bk: e.transpose(
                                pb[bk][:, c * 128:(c + 1) * 128], stage[:, (h4 * 4 + c) * 128:(h4 * 4 + c + 1) * 128], ident[:])
                                for c in range(4)], [b_stage, b_const], [b_pb[bk]])
                            P.op("vector", lambda e, h4=h4, bk=bk, tc=tc: e.tensor_copy(
                                out=cKT[:, h4 * 4:(h4 + 1) * 4, tc * 128:(tc + 1) * 128], in_=pbv(bk, 4, 128)), [b_pb[bk]], [b_kown])
                    cut(42, kind == "B")
                    Vh = [bigv(OFF_V + i * 2304, 18, 128) for i in range(2)]
                    Vhf = [bass.AP(bigf, (OFF_V + i * 2304) // 2, [[BIG_E // 2, 128], [64, 18], [1, 64]]) for i in range(2)]

                    def load_head(kvh, n):
                        i = n % 2
                        ksp, kloc = kvh // HPS, kvh % HPS
                        P.dma("sync", lambda e: e.dma_start(
                            out=bass.AP(kThf[i], 0, [[1024, 128], [256, 4], [1, 256]]),
                            in_=kx[(kind, "K", "all", ksp)].ap().rearrange("(r p) n -> p r n", p=128)[:, :, kloc * 256:(kloc + 1) * 256]),
                            [b_kvall[kind]], [b_kth[i]])
                        wr = [b_vh[i]] + ([b_vown] if n < 2 else [])
                        P.dma("sync", [lambda e, r=r, sp=sp: e.dma_start(
                            out=Vhf[i][:, 2 + 4 * r + sp * TPS:2 + 4 * r + (sp + 1) * TPS, :],
                            in_=kx[(kind, "V", "all", sp)].ap()[r * 128:(r + 1) * 128, :].rearrange(
                                "p (c hh w) -> p c hh w", c=TPS, hh=NK, w=64)[:, :, kvh, :])
                            for r in range(4) for sp in range(nsp)], [b_vall[kind]], wr)
                        P.dma("gpsimd", lambda e: e.dma_start(
                            out=Vh[i][:, 0:2, :], in_=cv_d.rearrange("(c p) n -> p c n", p=128)[:, :, kvh * 128:(kvh + 1) * 128]),
                            [], [b_vh[i]])
                    nload = [0]
                    load_head(0, 0)
                    nload[0] = 1
                    load_head(1, 1)
                    nload[0] = 2
                ckpt(mtag + "_xch")
                def att_core(rhs_ap, rbufs, ob, db, nq, chunks, kvb):
                    nch = len(chunks)
                    LA = 2
                    eis = {}
                    for step in range(nch + LA):
                        if step < nch:
                            ic = step
                            kTc = chunks[ic][0]
                            sbk = ic % 4
                            P.op("tensor", lambda e, kTc=kTc, sbk=sbk: e.matmul(
                                pb[sbk][:, 0:nq], lhsT=kTc, rhs=rhs_ap, start=True, stop=True),
                                kvb + rbufs, [b_pb[sbk]])
                            ei = E()
                            eis[ic] = ei
                            P.op("scalar", lambda e, ei=ei, sbk=sbk: e.activation(
                                out=ebuf[ei][:, 0:nq], in_=pb[sbk][:, 0:nq], func=AF.Exp, scale=sm_scale), [b_pb[sbk]], [b_eb[ei]])
                        ic = step - LA
                        if ic >= 0:
                            Vc = chunks[ic][1]
                            ei = eis[ic]
                            P.op("tensor", [lambda e, Vc=Vc, ei=ei, ic=ic: e.matmul(
                                pb[ob][:, 0:nq], lhsT=Vc, rhs=ebuf[ei][:, 0:nq], start=(ic == 0), stop=(ic == nch - 1)),
                                lambda e, ei=ei, ic=ic: e.matmul(
                                pb[db][:, 0:nq], lhsT=onesbf[:], rhs=ebuf[ei][:, 0:nq], start=(ic == 0), stop=(ic == nch - 1))],
                                kvb + [b_eb[ei], b_const], [b_pb[ob], b_pb[db]])

                def att_A(h, q0, nq, chunks, kvb):
                    ob, db = 4, 6
                    att_core(qT[:, h, q0:q0 + nq], [b_q], ob, db, nq, chunks, kvb)
                    ri = S()
                    P.op("vector", [lambda e: e.reciprocal(out=scr[ri][:, 0:nq], in_=pb[db][:, 0:nq]),
                                    lambda e: e.tensor_tensor(out=hT[:, h, q0:q0 + nq], in0=pb[ob][:, 0:nq],
                                                              in1=scr[ri][:, 0:nq], op=ALU.mult)],
                         [b_pb[ob], b_pb[db]], [b_scr[ri], b_h])

                def att_B(h, q0, nq, chunks, kvb):
                    nch = len(chunks)
                    LA = 1
                    eis = {}
                    P.drain("tensor")
                    for step in range(nch + LA):
                        if step < nch:
                            ic = step
                            kTc = chunks[ic][0]
                            for m in range(2):
                                sbk = (ic % 2) * 2 + m
                                P.op("tensor", lambda e, kTc=kTc, sbk=sbk, m=m: e.matmul(
                                    pb[sbk][:, 0:nq], lhsT=kTc[m * 64:(m + 1) * 64, :], rhs=qT[m * 64:(m + 1) * 64, h, q0:q0 + nq],
                                    start=True, stop=True), kvb + [b_q], [b_pb[sbk]])
                                ei = E()
                                eis[(ic, m)] = ei
                                P.op("scalar", lambda e, ei=ei, sbk=sbk: e.activation(
                                    out=ebuf[ei][:, 0:nq], in_=pb[sbk][:, 0:nq], func=AF.Exp, scale=sm_scale), [b_pb[sbk]], [b_eb[ei]])
                            P.drain("tensor")
                        ic = step - LA
                        if ic >= 0:
                            Vc = chunks[ic][1]
                            for m in range(2):
                                ei = eis[(ic, m)]
                                P.op("tensor", [lambda e, Vc=Vc, ei=ei, ic=ic, m=m: e.matmul(
                                    pb[4 + m][:, 0:nq], lhsT=Vc, rhs=ebuf[ei][:, 0:nq], start=(ic == 0), stop=(ic == nch - 1)),
                                    lambda e, ei=ei, ic=ic, m=m: e.matmul(
                                    pb[6 + m][:, 0:nq], lhsT=onesbf[:], rhs=ebuf[ei][:, 0:nq], start=(ic == 0), stop=(ic == nch - 1))],
                                    kvb + [b_eb[ei], b_const], [b_pb[4 + m], b_pb[6 + m]])
                            P.drain("tensor")
                    r0, r1, oi, si = S(), S(), S(), S()
                    P.op("vector", [lambda e: e.reciprocal(out=scr[r0][:, 0:nq], in_=pb[6][:, 0:nq]),
                                    lambda e: e.reciprocal(out=scr[r1][:, 0:nq], in_=pb[7][:, 0:nq]),
                                    lambda e: e.tensor_tensor(out=scr[r0][:, 0:nq], in0=pb[4][:, 0:nq], in1=scr[r0][:, 0:nq], op=ALU.mult),
                                    lambda e: e.tensor_tensor(out=scr[r1][:, 0:nq], in0=pb[5][:, 0:nq], in1=scr[r1][:, 0:nq], op=ALU.mult),
                                    lambda e: e.scalar_tensor_tensor(out=scr[oi][:, 0:nq], in0=scr[r1][:, 0:nq], scalar=neglam_col,
                                                                     in1=scr[r0][:, 0:nq], op0=ALU.mult, op1=ALU.add)],
                         [b_pb[4], b_pb[5], b_pb[6], b_pb[7], b_small], [b_scr[r0], b_scr[r1], b_scr[oi]])
                    P.op("scalar", lambda e: e.activation(out=scr[si][:, 0:nq], in_=scr[oi][:, 0:nq], func=AF.Square),
                         [b_scr[oi]], [b_scr[si]])
                    P.op("tensor", lambda e: e.matmul(pb[0][:, 0:nq], lhsT=ones32[:], rhs=scr[si][:, 0:nq], start=True, stop=True),
                         [b_scr[si], b_const], [b_pb[0]])
                    P.op("scalar", lambda e: e.activation(out=scr[si][:, 0:nq], in_=pb[0][:, 0:nq], func=AF.Sqrt,
                                                          bias=eps_col, scale=1.0 / 128), [b_pb[0], b_small], [b_scr[si]])
                    P.op("vector", [lambda e: e.reciprocal(out=scr[si][:, 0:nq], in_=scr[si][:, 0:nq]),
                                    lambda e: e.tensor_tensor(out=scr[oi][:, 0:nq], in0=scr[oi][:, 0:nq], in1=scr[si][:, 0:nq], op=ALU.mult)],
                         [b_scr[si], b_scr[oi]], [b_scr[si], b_scr[oi]])
                    P.op("scalar", lambda e: e.activation(out=hT[:, h, q0:q0 + nq], in_=scr[oi][:, 0:nq], func=AF.Identity,
                                                          scale=subg_col), [b_scr[oi], b_small], [b_h])

                for h in range(16):
                    kvh = h // 4 if kind == "A" else h
                    if sample:
                        n = kvh
                        i = n % 2
                        chunks = [(cKT[:, kvh, 0:128], Vh[i][:, 0, :]), (cKT[:, kvh, 128:256], Vh[i][:, 1, :])]
                        chunks += [(kTh[i][:, kc * 128:(kc + 1) * 128], Vh[i][:, 2 + kc, :]) for kc in range(16)]
                        groups = [(0, T, chunks, [b_kown, b_kth[i], b_vh[i]])]
                    else:
                        groups = []
                        for sq in range(2):
                            chunks = [(kTown[:, kvh, (2 * sq + c) * 128:(2 * sq + c + 1) * 128],
                                       Vown[:, 2 * sq + c, kvh * 128:(kvh + 1) * 128]) for c in range(2)]
                            groups.append((sq * 256, 256, chunks, [b_kown, b_vown]))
                    if False:
                        qi_ = h % 2
                        P.op("vector", [lambda e, h=h, qi_=qi_: e.tensor_copy(out=qpad[qi_][0:64, 0, :], in_=qT[0:64, h, :]),
                                        lambda e, h=h, qi_=qi_: e.tensor_copy(out=qpad[qi_][64:128, 1, :], in_=qT[64:128, h, :])],
                             [b_q], [b_qp[qi_]])
                    for (q0, nq, chunks, kvb) in groups:
                        if kind == "A":
                            att_A(h, q0, nq, chunks, kvb)
                        else:
                            att_B(h, q0, nq, chunks, kvb)
                    cut(36, h == 0)
                    cut(37, h == 1)
                    cut(38, h == 3)
                    cut(39, h == 7)
                    if sample:
                        last_of_kvh = (kind == "B") or (h % 4 == 3)
                        nxt = nload[0]
                        if last_of_kvh and nxt < NK:
                            load_head(nxt, nxt)
                            nload[0] += 1
                ckpt(mtag + "_att")
                for obk in range(4):
                    def loader(sl, obk=obk):
                        return [lambda e: e.dma_start(out=sv(sl, 16, 512),
                                                      in_=wo[:, obk * 512:(obk + 1) * 512].rearrange("(k p) n -> p k n", p=128))]
                    sl, sbuf = ws.next(loader)
                    wv = sv(sl, 16, 512)
                    for c4 in range(4):
                        fc = obk * 4 + c4
                        yb = c4 % 2
                        P.op("tensor", [lambda e, kc=kc, wv=wv, c4=c4, yb=yb: e.matmul(
                            pb[yb][:], lhsT=wv[:, kc, c4 * 128:(c4 + 1) * 128], rhs=hT[:, kc, :],
                            start=(kc == 0), stop=(kc == KC - 1)) for kc in range(KC)], [sbuf, b_h], [b_pb[yb]])
                        P.op("vector", lambda e, fc=fc, yb=yb: e.scalar_tensor_tensor(
                            out=xT[:, fc, :], in0=pb[yb][:], scalar=mcol(modG1, s, 2, fc, l, ci), in1=xT[:, fc, :],
                            op0=ALU.mult, op1=ALU.add), [b_pb[yb], b_x[fc], b_mod], [b_x[fc]])
                layernorm(l, s)

            for (ci, sample, src, dst, ob) in [(0, False, xp_d, yp_d, b_out["yp"]), (1, True, xs_d, ys_d, b_out["ys"])]:
                tag = "s" if sample else "p"
                load_x(src)
                ckpt(tag + "load")
                for l in range(2):
                    ffn(l, 0, ci, 0)
                    ckpt(tag + "ffn0_%d" % l)
                    mixer(l, ci, sample, "A" if l == 0 else "B")
                    ckpt(tag + "mix_%d" % l)
                    ffn(l, 1, ci, 2)
                    ckpt(tag + "ffn1_%d" % l)
                store_x(dst, ob)
                ckpt(tag + "store")
            P.wait_all("sync", list(b_out.values()))

        dry = WStream(Prog(), b_slot)
        for b in ([b_h, b_q, b_kown, b_vown, b_stage, b_mod, b_rope, b_const, b_small, b_cond, b_scT, b_mpart,
                   b_modin, b_modout] + b_x + b_act + b_slot + b_kth + b_vh + b_scr + b_eb + b_pb
                  + list(b_kvin.values()) + list(b_kvall.values()) + list(b_out.values())):
            pass
        try:
            program(dry.P, dry)
        except _Stop:
            pass
        allb = ([b_h, b_q, b_kown, b_vown, b_stage, b_mod, b_rope, b_const, b_small, b_cond, b_scT, b_mpart,
                 b_modin, b_modout] + b_x + b_act + b_slot + b_kth + b_vh + b_scr + b_eb + b_pb
                + list(b_kvin.values()) + list(b_kvall.values()) + list(b_out.values())
                + list(b_vin.values()) + list(b_vall.values()))
        for b in allb:
            b.w = None
            b.r = []
        P = Prog()
        ws = WStream(P, b_slot, dry.loaders)
        try:
            program(P, ws)
        except _Stop:
            P.wait_all("sync", allb)
        assert ws.i == len(dry.loaders), (ws.i, len(dry.loaders))
        P.emit(nc, st)
    return nc


def _rope_tables(qtr):
    n = np.arange(qtr * T, (qtr + 1) * T)
    row = (n // 64).astype(np.float32)
    col = (n % 64).astype(np.float32)
    out = []
    for dim in (128, 64):
        quarter = dim // 4
        inv = (np.float32(10000.0) ** (-np.arange(quarter, dtype=np.float32) / np.float32(quarter))).astype(np.float32)
        ang = np.concatenate([row[:, None] * inv, col[:, None] * inv], axis=-1).astype(np.float32)
        half = dim // 2
        cosT = np.cos(ang).astype(np.float32).T
        sinT = np.sin(ang).astype(np.float32).T
        reps = 128 // half
        out.append(np.tile(cosT, (reps, 1)))
        out.append(np.tile(sinT, (reps, 1)))
    return np.ascontiguousarray(np.concatenate(out, axis=1), dtype=np.float32)


def _rot_mats():
    out = []
    for half in (64, 32):
        R = np.zeros((128, 128), np.float32)
        for i in range(128):
            blk = (i // (2 * half)) * 2 * half
            r = i - blk
            if r < half:
                R[i, blk + r + half] = -1.0
            else:
                R[i, blk + r - half] = 1.0
        out.append(R.T)
    return np.ascontiguousarray(np.concatenate(out, axis=1), dtype=np.float32)


def kernel(x_prompt, x_sample, cache_a_k, cache_a_v, cache_b_k, cache_b_v, c, c_ctx,
           ada_w, ada_b, ln_g, ln_b, ffn_w_in, ffn_w_out,
           a_w_qkv, a_q_norm, a_k_norm, a_w_o,
           b_w_qkv, b_lambda, b_subln, b_w_o):
    f = lambda a: np.ascontiguousarray(np.asarray(a), dtype=np.float32)
    x_prompt, x_sample = f(x_prompt), f(x_sample)
    ada_w, ada_b = f(ada_w), f(ada_b)
    ln_g, ln_b = f(ln_g), f(ln_b)
    w_in, w_out = f(ffn_w_in), f(ffn_w_out)
    aqkv, awo, bqkv, bwo = f(a_w_qkv)[0], f(a_w_o)[0], f(b_w_qkv)[0], f(b_w_o)[0]
    c, c_ctx = f(c), f(c_ctx)
    lnT = np.stack([ln_g, ln_b], 0).reshape(2, 6, 16, 128).transpose(3, 0, 1, 2).reshape(128, 192)
    gains = np.stack([f(a_q_norm)[0], f(a_k_norm)[0], f(b_subln)[0]], 1)
    blam = np.tile(f(b_lambda).reshape(1, 256), (128, 1))
    rot = _rot_mats()
    in_maps = []
    for r in range(NCORES):
        b, qtr = r // 4, r % 4
        condT = np.stack([c_ctx.reshape(16, 128).T, c[b].reshape(16, 128).T], -1).reshape(128, 32)
        adaw_s = ada_w[:, :, qtr * 4608:(qtr + 1) * 4608]
        adab_s = ada_b[:, qtr * 4608:(qtr + 1) * 4608].reshape(2, 36, 128).transpose(2, 1, 0).reshape(128, 72)
        in_maps.append({
            "xp": x_prompt[2 * r:2 * r + 2].reshape(T, D),
            "xs": x_sample[b, qtr * T:(qtr + 1) * T],
            "cak": f(cache_a_k)[b, 0].reshape(256, 512), "cav": f(cache_a_v)[b, 0].reshape(256, 512),
            "cbk": f(cache_b_k)[b, 0].reshape(256, 2048), "cbv": f(cache_b_v)[b, 0].reshape(256, 2048),
            "condT": f(condT), "adaw": f(adaw_s), "adab": f(adab_s), "lnT": f(lnT),
            "w_in": w_in, "w_out": w_out, "a_qkv": aqkv, "a_wo": awo, "b_qkv": bqkv, "b_wo": bwo,
            "gains": f(gains), "blam": f(blam), "rope": _rope_tables(qtr), "rot": rot,
        })
    if KFAKE:
        for m in in_maps:
            for k in ("w_in", "w_out", "a_qkv", "a_wo", "b_qkv", "b_wo", "adaw"):
                del m[k]
    nc = build_nc()
    res = run_bass_kernel_spmd(nc, in_maps, core_ids=list(range(NCORES)))
    R = res.results
    y_prompt = np.concatenate([R[r]["yp"].reshape(2, 256, D) for r in range(NCORES)], 0)
    y_sample = np.stack([np.concatenate([R[b * 4 + q]["ys"] for q in range(4)], 0) for b in range(2)], 0)
    nak = np.concatenate([R[r]["nak"].reshape(2, 1, 256, 4, 128) for r in range(NCORES)], 0)
    nav = np.concatenate([R[r]["nav"].reshape(2, 1, 256, 4, 128) for r in range(NCORES)], 0)
    nbk = np.concatenate([R[r]["nbk"].reshape(2, 1, 256, 16, 2, 64) for r in range(NCORES)], 0)
    nbv = np.concatenate([R[r]["nbv"].reshape(2, 1, 256, 16, 128) for r in range(NCORES)], 0)
    return (y_prompt.astype(np.float32), y_sample.astype(np.float32), nak.astype(np.float32),
            nav.astype(np.float32), nbk.astype(np.float32), nbv.astype(np.float32))
```
